# Optimizing a Trainium2 kernel written in Bass

```python
import math
import jax, jax.numpy as jnp
from jax import lax
import numpy as np

D_MODEL = 2048
BATCH = 4
SEQ = 2048
DEPTH = 1

D_MIX = D_MODEL
DA_QK = 64
DA_V = 2 * DA_QK
DA_HEADS = D_MODEL // 256
DA_WIDTH = DA_HEADS * DA_V
ML_QK = 128
ML_V = 256
ML_HEADS = D_MODEL // 512
ML_WIDTH = ML_HEADS * ML_V
CONV_W = 4
ML_CHUNK = 64
GATE_CAP = 15.0
Q_BLOCK = 128
ROPE_THETA = 10000.0
PEER_HEADS = 8
PEER_NKEYS = 128
PEER_EXPERTS = PEER_NKEYS * PEER_NKEYS
PEER_QDIM = 256
PEER_TOPK = 16
PEER_TOK_BLOCK = 128
EPS = 1e-6

PROJ_SIZES = [2 * DA_HEADS * DA_QK, 2 * DA_HEADS * DA_QK, DA_WIDTH,
              ML_HEADS * ML_QK, ML_HEADS * ML_QK, ML_WIDTH, ML_WIDTH,
              ML_HEADS, ML_HEADS]
PROJ_DIM = sum(PROJ_SIZES)
PROJ_SPLITS = [int(s) for s in np.cumsum(PROJ_SIZES)[:-1]]

kernel_name = 'hymba_diffattn_mlstm_peer_adaln'


def rms_norm(x, g):
    xf = x.astype(jnp.float32)
    y = xf * lax.rsqrt(jnp.mean(xf * xf, axis=-1, keepdims=True) + EPS)
    return (y * g.astype(jnp.float32)).astype(x.dtype)


def modulate(h, shift, scale):
    return h * (1 + scale[:, None, :]) + shift[:, None, :]


def rope(x, pos):
    d = x.shape[-1]
    half = d // 2
    inv = ROPE_THETA ** (-jnp.arange(half, dtype=jnp.float32) / half)
    ang = pos.astype(jnp.float32)[:, None] * inv[None, :]
    cos = jnp.cos(ang)[None, :, None, :]
    sin = jnp.sin(ang)[None, :, None, :]
    xf = x.astype(jnp.float32)
    x1, x2 = xf[..., :half], xf[..., half:]
    return jnp.concatenate([x1 * cos - x2 * sin, x2 * cos + x1 * sin], axis=-1).astype(x.dtype)


def causal_dwconv(x, w, b):
    C = x.shape[-1]
    y = lax.conv_general_dilated(x, w[:, None, :].astype(x.dtype), window_strides=(1,),
                                 padding=[(CONV_W - 1, 0)],
                                 dimension_numbers=('NWC', 'WIO', 'NWC'),
                                 feature_group_count=C)
    return y + b


def differential_attention(q, k, v, lam):
    B, S = q.shape[0], q.shape[1]
    nqb = S // Q_BLOCK
    qb = q.reshape(B, nqb, Q_BLOCK, 2 * DA_HEADS, DA_QK).transpose(1, 0, 2, 3, 4)
    starts = jnp.arange(nqb) * Q_BLOCK
    kpos = jnp.arange(S)
    scale = DA_QK ** -0.5

    def one_block(args):
        q_blk, start = args
        s = jnp.einsum('bqhd,bkhd->bhqk', q_blk, k).astype(jnp.float32) * scale
        qpos = start + jnp.arange(Q_BLOCK)
        s = jnp.where(kpos[None, :] <= qpos[:, None], s, -jnp.inf)
        p = jax.nn.softmax(s, axis=-1).reshape(B, DA_HEADS, 2, Q_BLOCK, S)
        a = p[:, :, 0] - lam * p[:, :, 1]
        return jnp.einsum('bhqk,bkhe->bqhe', a.astype(v.dtype), v)

    o = lax.map(one_block, (qb, starts))
    return o.transpose(1, 0, 2, 3, 4).reshape(B, S, DA_HEADS, DA_V)


def mlstm_chunkwise(q, k, v, li, lf):
    B, S, H, dk = q.shape
    dv = v.shape[-1]
    nc = S // ML_CHUNK
    L = ML_CHUNK

    def to_chunks4(t):
        return t.reshape(B, nc, L, H, t.shape[-1]).transpose(1, 0, 3, 2, 4)

    def to_chunks3(t):
        return t.reshape(B, nc, L, H).transpose(1, 0, 3, 2)

    tri = jnp.tril(jnp.ones((L, L), dtype=bool))

    def body(carry, xs):
        C, n, m = carry
        qc, kc, vc, lic, lfc = xs
        b = jnp.cumsum(lfc, axis=-1)
        D = jnp.where(tri, b[..., :, None] - b[..., None, :] + lic[..., None, :], -jnp.inf)
        inter = b + m[..., None]
        m_j = jnp.maximum(inter, jnp.max(D, axis=-1))
        w_intra = jnp.exp(D - m_j[..., None])
        w_inter = jnp.exp(inter - m_j)
        qk = jnp.einsum('bhjd,bhsd->bhjs', qc, kc) * w_intra
        num = w_inter[..., None] * jnp.einsum('bhjd,bhde->bhje', qc, C) + \
            jnp.einsum('bhjs,bhse->bhje', qk, vc)
        den = w_inter * jnp.einsum('bhjd,bhd->bhj', qc, n) + jnp.sum(qk, axis=-1)
        h = num / jnp.maximum(jnp.abs(den), jnp.exp(-m_j))[..., None]
        bL = b[..., -1]
        logw = bL[..., None] - b + lic
        m_new = jnp.maximum(bL + m, jnp.max(logw, axis=-1))
        decay = jnp.exp(bL + m - m_new)
        ws = jnp.exp(logw - m_new[..., None])
        C_new = decay[..., None, None] * C + jnp.einsum('bhs,bhsd,bhse->bhde', ws, kc, vc)
        n_new = decay[..., None] * n + jnp.einsum('bhs,bhsd->bhd', ws, kc)
        return (C_new, n_new, m_new), h

    init = (jnp.zeros((B, H, dk, dv), jnp.float32),
            jnp.zeros((B, H, dk), jnp.float32),
            jnp.zeros((B, H), jnp.float32))
    xs = (to_chunks4(q), to_chunks4(k), to_chunks4(v), to_chunks3(li), to_chunks3(lf))
    _, hs = lax.scan(body, init, xs)
    return hs.transpose(1, 0, 3, 2, 4).reshape(B, S, H, dv)


def softcap(x):
    return GATE_CAP * jnp.tanh(x / GATE_CAP)


def peer(h, w_pq, sub_keys, peer_u, peer_v):
    B, S, D = h.shape
    T = B * S
    ht = h.reshape(T, D)
    q = (ht @ w_pq).reshape(T, PEER_HEADS, 2, PEER_QDIM // 2)
    s = jnp.einsum('thcd,hcnd->thcn', q, sub_keys).astype(jnp.float32)
    s_top, i_top = lax.top_k(s, PEER_TOPK)
    cand = (s_top[:, :, 0, :, None] + s_top[:, :, 1, None, :]).reshape(T, PEER_HEADS, PEER_TOPK * PEER_TOPK)
    cand_idx = (i_top[:, :, 0, :, None] * PEER_NKEYS + i_top[:, :, 1, None, :]).reshape(T, PEER_HEADS, PEER_TOPK * PEER_TOPK)
    sc, pos = lax.top_k(cand, PEER_TOPK)
    idx = jnp.take_along_axis(cand_idx, pos, axis=-1)
    g = jax.nn.softmax(sc, axis=-1)
    nb = T // PEER_TOK_BLOCK

    def one_block(args):
        h_b, idx_b, g_b = args
        u_sel = jnp.take(peer_u, idx_b, axis=0)
        a = jax.nn.gelu(jnp.einsum('td,thkd->thk', h_b, u_sel).astype(jnp.float32), approximate=False)
        w = (g_b * a).astype(h.dtype)
        v_sel = jnp.take(peer_v, idx_b, axis=0)
        return jnp.einsum('thk,thkd->td', w, v_sel)

    y = lax.map(one_block, (ht.reshape(nb, PEER_TOK_BLOCK, D),
                            idx.reshape(nb, PEER_TOK_BLOCK, PEER_HEADS, PEER_TOPK),
                            g.reshape(nb, PEER_TOK_BLOCK, PEER_HEADS, PEER_TOPK)))
    return y.reshape(B, S, D)


def setup_inputs(seed: int = 0) -> dict:
    key = jax.random.key(seed)
    ks = jax.random.split(key, 26)
    f32 = jnp.float32
    D = D_MODEL
    nrm = lambda k, shp, s: jax.random.normal(k, shp, f32) * s
    return {
        'x': nrm(ks[0], (BATCH, SEQ, D), 1.0),
        'c': nrm(ks[1], (BATCH, D), 1.0),
        'w_ada': nrm(ks[2], (DEPTH, D, 6 * D), 0.5 * D ** -0.5),
        'b_ada': nrm(ks[3], (DEPTH, 6 * D), 0.02),
        'g_mix': 1.0 + nrm(ks[4], (DEPTH, D), 0.02),
        'w_in': nrm(ks[5], (DEPTH, D, PROJ_DIM), D ** -0.5),
        'conv_w': nrm(ks[6], (DEPTH, CONV_W, 2 * ML_HEADS * ML_QK), CONV_W ** -0.5),
        'conv_b': nrm(ks[7], (DEPTH, 2 * ML_HEADS * ML_QK), 0.02),
        'b_igate': -3.0 + nrm(ks[8], (DEPTH, ML_HEADS), 0.1),
        'b_fgate': jnp.broadcast_to(jnp.linspace(3.0, 6.0, ML_HEADS, dtype=f32), (DEPTH, ML_HEADS)) + nrm(ks[9], (DEPTH, ML_HEADS), 0.1),
        'lambda_q1': nrm(ks[10], (DEPTH, DA_QK), 0.1),
        'lambda_k1': nrm(ks[11], (DEPTH, DA_QK), 0.1),
        'lambda_q2': nrm(ks[12], (DEPTH, DA_QK), 0.1),
        'lambda_k2': nrm(ks[13], (DEPTH, DA_QK), 0.1),
        'da_norm': 1.0 + nrm(ks[14], (DEPTH, DA_V), 0.02),
        'ml_norm': 1.0 + nrm(ks[15], (DEPTH, ML_V), 0.02),
        'w_out': nrm(ks[16], (DEPTH, D_MIX, D), D_MIX ** -0.5),
        'g_ffn': 1.0 + nrm(ks[17], (DEPTH, D), 0.02),
        'w_pq': nrm(ks[18], (DEPTH, D, PEER_HEADS * PEER_QDIM), D ** -0.5),
        'sub_keys': nrm(ks[19], (DEPTH, PEER_HEADS, 2, PEER_NKEYS, PEER_QDIM // 2), (PEER_QDIM // 2) ** -0.5),
        'peer_u': nrm(ks[20], (DEPTH, PEER_EXPERTS, D), D ** -0.5),
        'peer_v': nrm(ks[21], (DEPTH, PEER_EXPERTS, D), PEER_HEADS ** -0.5),
        'w_ada_final': nrm(ks[22], (D, 2 * D), 0.5 * D ** -0.5),
        'b_ada_final': nrm(ks[23], (2 * D,), 0.02),
        'g_final': 1.0 + nrm(ks[24], (D,), 0.02),
    }


def reference(x, c, w_ada, b_ada, g_mix, w_in, conv_w, conv_b, b_igate, b_fgate,
              lambda_q1, lambda_k1, lambda_q2, lambda_k2, da_norm, ml_norm, w_out,
              g_ffn, w_pq, sub_keys, peer_u, peer_v, w_ada_final, b_ada_final, g_final):
    B, S, D = x.shape
    pos = jnp.arange(S)
    cs = jax.nn.silu(c.astype(jnp.float32)).astype(x.dtype)
    for l in range(DEPTH):
        mod = (cs @ w_ada[l] + b_ada[l]).reshape(B, 6, D)
        sh_m, sc_m, gt_m, sh_f, sc_f, gt_f = [mod[:, i] for i in range(6)]

        h = modulate(rms_norm(x, g_mix[l]), sh_m, sc_m)
        p = jnp.einsum('bsd,dp->bsp', h, w_in[l])
        da_q, da_k, da_v, ml_q, ml_k, ml_v, ml_o, ml_i, ml_f = jnp.split(p, PROJ_SPLITS, axis=-1)

        lam_init = 0.8 - 0.6 * math.exp(-0.3 * l)
        lam = (jnp.exp(jnp.sum(lambda_q1[l].astype(jnp.float32) * lambda_k1[l].astype(jnp.float32)))
               - jnp.exp(jnp.sum(lambda_q2[l].astype(jnp.float32) * lambda_k2[l].astype(jnp.float32)))
               + lam_init)
        qa = rope(da_q.reshape(B, S, 2 * DA_HEADS, DA_QK), pos)
        ka = rope(da_k.reshape(B, S, 2 * DA_HEADS, DA_QK), pos)
        va = da_v.reshape(B, S, DA_HEADS, DA_V)
        oa = differential_attention(qa, ka, va, lam)
        oa = (rms_norm(oa, da_norm[l]) * (1.0 - lam_init)).reshape(B, S, DA_WIDTH)

        qk_m = jax.nn.silu(causal_dwconv(jnp.concatenate([ml_q, ml_k], axis=-1), conv_w[l], conv_b[l]))
        qm, km = jnp.split(qk_m, 2, axis=-1)
        qm = qm.reshape(B, S, ML_HEADS, ML_QK).astype(jnp.float32)
        km = km.reshape(B, S, ML_HEADS, ML_QK).astype(jnp.float32) * (ML_QK ** -0.5)
        vm = ml_v.reshape(B, S, ML_HEADS, ML_V).astype(jnp.float32)
        li = softcap((ml_i + b_igate[l]).astype(jnp.float32))
        lf = jax.nn.log_sigmoid(softcap((ml_f + b_fgate[l]).astype(jnp.float32)))
        hm = mlstm_chunkwise(qm, km, vm, li, lf).astype(x.dtype)
        om = rms_norm(hm, ml_norm[l]).reshape(B, S, ML_WIDTH) * jax.nn.sigmoid(ml_o)

        mixed = jnp.einsum('bsm,md->bsd', jnp.concatenate([oa, om], axis=-1), w_out[l])
        x = x + gt_m[:, None, :] * mixed

        h2 = modulate(rms_norm(x, g_ffn[l]), sh_f, sc_f)
        x = x + gt_f[:, None, :] * peer(h2, w_pq[l], sub_keys[l], peer_u[l], peer_v[l])

    mod_o = (cs @ w_ada_final + b_ada_final).reshape(B, 2, D)
    return modulate(rms_norm(x, g_final), mod_o[:, 0], mod_o[:, 1])
```

```python
import numpy as np
import concourse.bass as bass
import concourse.mybir as mybir
from concourse.bass_utils import run_bass_kernel_spmd

F32 = mybir.dt.float32
BF16 = mybir.dt.bfloat16
AF = mybir.ActivationFunctionType
ALU = mybir.AluOpType
AX = mybir.AxisListType

EPOCH = 30000


class Res:
    def __init__(self, name, t=None):
        self.name = name
        self.t = t
        self.w = None
        self.rs = []
        self.dsem = None
        self.dval = 0
        self.psum = False


class Prog:
    ENG = ('pe', 'act', 'dve', 'pool', 'sp')

    def __init__(self, nc):
        self.nc = nc
        self.q = {e: [] for e in self.ENG}
        self.cnt = {e: 0 for e in self.ENG}
        self.wm = {e: {} for e in self.ENG}
        self.sems = {}
        self.out_tokens = []
        self._stack = []
        self.nres = 0

    def sb(self, name, shape, dt):
        g = self.nc.sbuf_tensor(name, shape, dt)
        t = g.__enter__()
        self._stack.append(g)
        return Res(name, t)

    def ps(self, name, shape, dt):
        g = self.nc.psum_tensor(name, shape, dt)
        t = g.__enter__()
        self._stack.append(g)
        return Res(name, t)

    def arena(self, nbytes):
        g = self.nc.sbuf_tensor("arena", [128, nbytes], mybir.dt.uint8)
        self.ar = g.__enter__()
        self._stack.append(g)
        self.carved = []
        g2 = self.nc.psum_tensor("psar", [128, 4096], F32)
        self.par = g2.__enter__()
        self._stack.append(g2)
        self.pcarved = []

    def _alias(self, lst, r, off, nb):
        for (o, n, old) in lst:
            if o < off + nb and off < o + n:
                if old.w is not None:
                    r.rs.append(old.w)
                r.rs.extend(old.rs)
        lst.append((off, nb, r))

    def carve(self, name, off, shape, dt, parts=128):
        esz = 4 if dt == F32 else 2
        n = 1
        for d in shape[1:]:
            n *= d
        nb = n * esz
        assert off % 4 == 0 and off + nb <= self.ar.shape[1], (name, off, nb)
        t = self.ar[0:shape[0], off:off + nb].bitcast(dt)
        if len(shape) == 3:
            t = t.rearrange("p (a b) -> p a b", a=shape[1])
        elif len(shape) == 4:
            t = t.rearrange("p (a b c) -> p a b c", a=shape[1], b=shape[2])
        r = Res(name, t)
        self._alias(self.carved, r, off, nb)
        return r

    def pcarve(self, name, off, shape, dt):
        esz = 4 if dt == F32 else 2
        n = 1
        for d in shape[1:]:
            n *= d
        nb = n * esz
        assert off % 4 == 0 and off + nb <= 16384, (name, off, nb)
        t = self.par[0:shape[0], off // 4:(off + nb) // 4]
        if dt != F32:
            t = t.bitcast(dt)
        if len(shape) == 3:
            t = t.rearrange("p (a b) -> p a b", a=shape[1])
        elif len(shape) == 4:
            t = t.rearrange("p (a b c) -> p a b c", a=shape[1], b=shape[2])
        r = Res(name, t)
        r.psum = True
        self._alias(self.pcarved, r, off, nb)
        return r

    def sem(self, key):
        if key not in self.sems:
            g = self.nc.semaphore("s_%s_%s" % (str(key[0]), str(key[1])))
            h = g.__enter__()
            self._stack.append(g)
            self.sems[key] = h
        return self.sems[key]

    def _tok_semval(self, tok):
        if tok[0] == 'c':
            _, e, n = tok
            ep = (n - 1) // EPOCH
            return self.sem((e, ep)), (n - 1) % EPOCH + 1
        _, key, v = tok
        return self.sem(key), v

    def _deps(self, eng, reads, writes):
        need = []
        for r in reads:
            if r.w is not None:
                need.append(r.w)
            if r.psum:
                need.extend(t for t in r.rs if t[0] == 'c' and t[1] != eng)
        for w in writes:
            if w.w is not None:
                need.append(w.w)
            need.extend(w.rs)
        best = {}
        for tok in need:
            if tok[0] == 'c':
                if tok[1] == eng and eng == 'pe':
                    continue
                k = ('c', tok[1])
                v = tok[2]
            else:
                k = ('d', tok[1])
                v = tok[2]
            if self.wm[eng].get(k, 0) >= v:
                continue
            if best.get(k, 0) < v:
                best[k] = v
        waits = []
        for k, v in best.items():
            self.wm[eng][k] = v
            if k[0] == 'c':
                waits.append(('c', k[1], v))
            else:
                waits.append(('d', k[1], v))
        return waits

    def _mark(self, tok, reads, writes):
        for r in reads:
            r.rs.append(tok)
        for w in writes:
            w.w = tok
            w.rs = []

    def op(self, eng, fn, reads=(), writes=()):
        waits = [self._tok_semval(t) for t in self._deps(eng, reads, writes)]
        self.cnt[eng] += 1
        tok = ('c', eng, self.cnt[eng])
        sem, val = self._tok_semval(tok)
        self.wm[eng][('c', eng)] = max(self.wm[eng].get(('c', eng), 0), 0)

        def run(e, fn=fn, waits=waits, sem=sem):
            for s, v in waits:
                e.wait_ge(s, v)
            ins = fn(e)
            ins.then_inc(sem, 1)
        self.q[eng].append(run)
        self._mark(tok, reads, writes)
        return tok

    def dma(self, eng, out, in_, reads=(), writes=(), dres=None, is_out=False, n=1, fn=None):
        waits = [self._tok_semval(t) for t in self._deps(eng, reads, writes)]
        if dres.dsem is None:
            self.nres += 1
            dres.dsem = ('dma', self.nres)
        dres.dval += 16 * n
        tok = ('d', dres.dsem, dres.dval)
        sem, _ = self._tok_semval(tok)

        def run(e, waits=waits, sem=sem, fn=fn, out=out, in_=in_):
            for s, v in waits:
                e.wait_ge(s, v)
            if fn is None:
                e.dma_start(out=out, in_=in_).then_inc(sem, 16)
            else:
                for ins in fn(e):
                    ins.then_inc(sem, 16)
        self.q[eng].append(run)
        self._mark(tok, reads, writes)
        if is_out:
            self.out_tokens.append(tok)
        return tok

    def finish(self):
        best = {}
        for tok in self.out_tokens:
            best[tok[1]] = max(best.get(tok[1], 0), tok[2])
        waits = [self._tok_semval(('d', k, v)) for k, v in best.items()]

        def run(e, waits=waits):
            for s, v in waits:
                e.wait_ge(s, v)
        self.q['sp'].append(run)
        nc = self.nc
        q = self.q
        with nc.Block() as block:
            @block.sync
            def _(e):
                for f in q['sp']:
                    f(e)

            @block.scalar
            def _(e):
                for f in q['act']:
                    f(e)

            @block.vector
            def _(e):
                for f in q['dve']:
                    f(e)

            @block.gpsimd
            def _(e):
                for f in q['pool']:
                    f(e)

            @block.tensor
            def _(e):
                for f in q['pe']:
                    f(e)
        for g in reversed(self._stack):
            g.__exit__(None, None, None)


D = 2048
S = 2048
NOWN = 1024
EPS = 1e-6
PROJ = 6152
OFF_DAQ, OFF_DAK, OFF_DAV = 0, 1024, 2048
OFF_MLQ, OFF_MLK, OFF_MLV, OFF_MLO, OFF_MLG = 3072, 3584, 4096, 5120, 6144
ARENA = 207872
R_H, R_M, R_W, R_C = 0, 65536, 98304, 147456
R_T = R_C + 20480
R_T2 = R_T + 8448
C_IDENT, C_TRIBF, C_TRIUF, C_ONESF, C_COS, C_SIN, C_VALID = 0, 256, 512, 1024, 1536, 3584, 5632
C_MODFM, C_GMIX, C_GFFN, C_CSBF, C_CONVW, C_CONVB, C_BGATE, C_LAM = 5696, 5952, 6016, 6080, 6144, 6272, 6304, 6336
C_DANORM, C_MLNORM, C_GTM, C_GTF, C_MISC = 6400, 6912, 7936, 16128, 20224


def build_program(stage=99, dbg=None, peer_rows=16384):
    nc = bass.Bass("TRN2", target_bir_lowering=False)

    def din(name, shape, dt=F32):
        return nc.dram_tensor(name, list(shape), dt, kind="ExternalInput").ap()

    xw = din("xw", [S, D])
    cfm = din("cfm", [128, 16])
    w_ada = din("w_ada", [D, 6 * D])
    b_ada = din("b_ada", [1, 6 * D])
    w_adaf = din("w_adaf", [D, 2 * D])
    b_adaf = din("b_adaf", [1, 2 * D])
    gfin = din("gfin", [1, D])
    gmix_fm = din("gmix_fm", [128, 16])
    gffn_fm = din("gffn_fm", [128, 16])
    w_in = din("w_in", [D, PROJ])
    convw = din("convw", [128, 8, 4])
    convb = din("convb", [128, 8])
    bgate = din("bgate", [128, 8])
    lam4 = din("lam4", [128, 4, 64])
    danorm = din("danorm", [128, 128])
    mlnorm = din("mlnorm", [128, 256])
    w_out = din("w_out", [D, D])
    w_pq = din("w_pq", [D, D])
    subkT = din("subkT", [16, 128, 128])
    peer_u = din("peer_u", [peer_rows, D])
    peer_v = din("peer_v", [peer_rows, D])
    cosT = din("cosT", [128, 16, 32])
    sinT = din("sinT", [128, 16, 32])
    valid_d = din("valid", [128, 16])
    validrow_d = din("validrow", [128, S], BF16)
    ident_d = din("ident", [128, 128], BF16)
    tribf_d = din("tribf", [128, 128], BF16)
    triuf_d = din("triuf", [128, 128])
    y = nc.dram_tensor("y", [NOWN, D], F32, kind="ExternalOutput").ap()
    dbg_t = None
    if dbg is not None:
        dbg_t = nc.dram_tensor("dbg", list(dbg), F32, kind="ExternalOutput").ap()

    P = Prog(nc)
    P.arena(ARENA)
    LAM_INIT = 0.8 - 0.6 * 1.0

    def C(name, off, shape, dt):
        return P.carve(name, R_C + off, shape, dt)

    ident = C("ident", C_IDENT, [128, 128], BF16)
    tribf = C("tribf", C_TRIBF, [128, 128], BF16)
    triuf = C("triuf", C_TRIUF, [128, 128], F32)
    onesf = C("onesf", C_ONESF, [128, 128], F32)
    cos = C("cos", C_COS, [128, 16, 32], F32)
    sin = C("sin", C_SIN, [128, 16, 32], F32)
    valid = C("valid", C_VALID, [128, 16], F32)
    modfm = C("modfm", C_MODFM, [128, 4, 16], F32)
    gmix = C("gmix", C_GMIX, [128, 16], F32)
    gffn = C("gffn", C_GFFN, [128, 16], F32)
    csbf = C("csbf", C_MISC + 64, [128, 16], BF16)
    cw = C("convw", C_CONVW, [128, 8, 4], F32)
    cb = C("convb", C_CONVB, [128, 8], F32)
    bg = C("bgate", C_BGATE, [128, 8], F32)
    lam = C("lam", C_LAM, [128, 4], F32)
    dan = C("danorm", C_DANORM, [128, 128], F32)
    mln = C("mlnorm", C_MLNORM, [128, 256], F32)
    gtm = C("gtm", C_GTM, [128, 2048], F32)
    gtf = C("gtf", C_GTF, [128, 2048], BF16)
    misc = C("misc", C_MISC, [128, 64], F32)

    for (r, d) in ((ident, ident_d), (tribf, tribf_d), (triuf, triuf_d), (cos, cosT), (sin, sinT),
                   (valid, valid_d), (gmix, gmix_fm), (gffn, gffn_fm), (cw, convw), (cb, convb),
                   (bg, bgate), (dan, danorm), (mln, mlnorm)):
        P.dma('sp', r.t, d, writes=[r], dres=r)
    P.op('dve', lambda e: e.memset(onesf.t, 1.0), writes=[onesf])
    P.op('dve', lambda e: e.tensor_scalar(out=dan.t, in0=dan.t, scalar1=1.0 - LAM_INIT, scalar2=None, op0=ALU.mult),
         reads=[dan], writes=[dan])

    wring = [P.carve("w%d" % i, R_W + 16384 * i, [128, 16, 512], BF16) for i in range(3)]
    wstate = {'i': 0}

    def load_w(src, c0, ncol):
        r = wring[wstate['i'] % 3]
        wstate['i'] += 1
        P.dma('pool', r.t[:, :, 0:ncol], src[:, c0:c0 + ncol].rearrange("(k p) c -> p k c", p=128),
              writes=[r], dres=r)
        return r

    def PB(name, bank, shape, dt=F32, boff=0):
        return P.pcarve(name, bank * 2048 + boff, shape, dt)

    t_c = P.carve("t_c", R_T, [128, 16], F32)
    P.dma('sp', t_c.t, cfm, writes=[t_c], dres=t_c)
    P.op('act', lambda e: e.activation(out=csbf.t, in_=t_c.t, func=AF.Silu), reads=[t_c], writes=[csbf])
    brow = P.carve("brow", R_T + 35072, [1, 512], F32)
    mrow = P.carve("mrow", R_T + 37120, [1, 512], F32)
    one1 = P.carve("one1", R_T + 39168, [1, 128], F32)
    P.op('dve', lambda e: e.memset(one1.t, 1.0), writes=[one1])
    ps_row = PB("ps_row", 0, [1, 512])
    ps_fm = PB("ps_fm", 1, [128, 4])
    ps_bc = PB("ps_bc", 2, [128, 512])

    def ada_group(wsrc, bsrc, g):
        w = load_w(wsrc, g * 512, 512)
        P.dma('sp', brow.t, bsrc[:, g * 512:(g + 1) * 512], writes=[brow], dres=brow)

        def mm(e, w=w):
            ins = None
            for k in range(16):
                ins = e.matmul(ps_row.t, lhsT=csbf.t[:, k:k + 1], rhs=w.t[:, k, :], start=(k == 0), stop=(k == 15))
            return ins
        P.op('pe', mm, reads=[csbf, w], writes=[ps_row])
        P.op('dve', lambda e: e.tensor_tensor(out=mrow.t, in0=ps_row.t, in1=brow.t, op=ALU.add),
             reads=[ps_row, brow], writes=[mrow])

    def row_to_fm(dst_ap_fn, dst_res, add1=False):
        def mm(e):
            ins = None
            for c in range(4):
                ins = e.matmul(ps_fm.t[:, c:c + 1], lhsT=mrow.t[0:1, c * 128:(c + 1) * 128], rhs=one1.t[0:1, 0:1],
                               start=True, stop=True)
            return ins
        P.op('pe', mm, reads=[mrow, one1], writes=[ps_fm])
        if add1:
            P.op('dve', lambda e: e.tensor_scalar(out=dst_ap_fn(), in0=ps_fm.t, scalar1=1.0, scalar2=None, op0=ALU.add),
                 reads=[ps_fm], writes=[dst_res])
        else:
            P.op('dve', lambda e: e.tensor_copy(out=dst_ap_fn(), in_=ps_fm.t), reads=[ps_fm], writes=[dst_res])

    def row_to_bc(row_res, row_ap, dst_ap, dst_res, extra_reads=()):
        P.op('pe', lambda e: e.matmul(ps_bc.t, lhsT=one1.t[0:1, 0:128], rhs=row_ap, start=True, stop=True),
             reads=[row_res, one1], writes=[ps_bc])
        P.op('act', lambda e: e.copy(out=dst_ap, in_=ps_bc.t), reads=[ps_bc], writes=[dst_res])

    def ada_do(g):
        ada_group(w_ada, b_ada, g)
        vec, sub = g // 4, g % 4
        if vec in (0, 1, 3, 4):
            slot = {0: 0, 1: 1, 3: 2, 4: 3}[vec]
            row_to_fm(lambda slot=slot, sub=sub: modfm.t[:, slot, sub * 4:(sub + 1) * 4], modfm, add1=(vec in (1, 4)))
        elif vec == 2:
            row_to_bc(mrow, mrow.t[0:1, :], gtm.t[:, sub * 512:(sub + 1) * 512], gtm)
        else:
            row_to_bc(mrow, mrow.t[0:1, :], gtf.t[:, sub * 512:(sub + 1) * 512], gtf)
    for g in range(8):
        ada_do(g)
    P.op('dve', lambda e: e.tensor_tensor(out=modfm.t[:, 1, :], in0=modfm.t[:, 1, :], in1=gmix.t, op=ALU.mult),
         reads=[modfm, gmix], writes=[modfm])

    t_l4 = P.carve("t_l4", R_T + 20480, [128, 4, 64], F32)
    t_lp = P.carve("t_lp", R_T + 20480 + 1024, [128, 2, 64], F32)
    P.dma('sp', t_l4.t, lam4, writes=[t_l4], dres=t_l4)
    P.op('dve', lambda e: e.tensor_tensor(out=t_lp.t[:, 0, :], in0=t_l4.t[:, 0, :], in1=t_l4.t[:, 1, :], op=ALU.mult),
         reads=[t_l4], writes=[t_lp])
    P.op('dve', lambda e: e.tensor_tensor(out=t_lp.t[:, 1, :], in0=t_l4.t[:, 2, :], in1=t_l4.t[:, 3, :], op=ALU.mult),
         reads=[t_l4, t_lp], writes=[t_lp])
    P.op('dve', lambda e: e.tensor_reduce(out=lam.t[:, 0:2], in_=t_lp.t, axis=AX.X, op=ALU.add),
         reads=[t_lp], writes=[lam])
    P.op('act', lambda e: e.activation(out=lam.t[:, 2:4], in_=lam.t[:, 0:2], func=AF.Exp), reads=[lam], writes=[lam])
    P.op('dve', lambda e: e.tensor_tensor(out=lam.t[:, 0:1], in0=lam.t[:, 2:3], in1=lam.t[:, 3:4], op=ALU.subtract),
         reads=[lam], writes=[lam])
    P.op('dve', lambda e: e.tensor_scalar(out=lam.t[:, 0:1], in0=lam.t[:, 0:1], scalar1=LAM_INIT, scalar2=None, op0=ALU.add),
         reads=[lam], writes=[lam])

    xn = P.carve("xn", R_T, [128, 2048], BF16)
    junk = P.carve("junk", R_T + 4096, [128, 2048], BF16)
    ss = P.carve("ss", R_T + 8192, [128, 4], F32)
    xt = [P.carve("xt%d" % i, R_T2 + 8192 * i, [128, 2048], F32) for i in range(2)]
    ps_tr = [PB("ps_tr%d" % i, 6 + i, [128, 8, 128], BF16) for i in range(2)]

    def norm_to_fm(src_ap, src_res, dstT, tcol, slot_sh, slot_sc, tag):
        P.op('act', lambda e: e.activation(out=junk.t, in_=src_ap, func=AF.Square, accum_out=ss.t[:, 0:1]),
             reads=[src_res], writes=[junk, ss])
        P.op('dve', lambda e: e.tensor_scalar(out=ss.t[:, 1:2], in0=ss.t[:, 0:1], scalar1=1.0 / D, scalar2=EPS,
                                              op0=ALU.mult, op1=ALU.add), reads=[ss], writes=[ss])
        P.op('act', lambda e: e.activation(out=ss.t[:, 3:4], in_=ss.t[:, 1:2], func=AF.Sqrt), reads=[ss], writes=[ss])
        P.op('dve', lambda e: e.reciprocal(out=ss.t[:, 2:3], in_=ss.t[:, 3:4]), reads=[ss], writes=[ss])
        P.op('dve', lambda e: e.tensor_scalar(out=xn.t, in0=src_ap, scalar1=ss.t[:, 2:3], scalar2=None, op0=ALU.mult),
             reads=[src_res, ss], writes=[xn])
        for half in range(2):
            pt = ps_tr[half]

            def tr(e, half=half, pt=pt):
                ins = None
                for kk in range(8):
                    k = half * 8 + kk
                    ins = e.transpose(pt.t[:, kk, :], xn.t[:, k * 128:(k + 1) * 128], ident.t)
                return ins
            P.op('pe', tr, reads=[xn, ident], writes=[pt])
            for kk in range(8):
                k = half * 8 + kk
                if kk % 2 == 0:
                    P.op('dve', lambda e, k=k, kk=kk, pt=pt: e.tensor_scalar(
                        out=dstT.t[:, k, tcol:tcol + 128], in0=pt.t[:, kk, :],
                        scalar1=modfm.t[:, slot_sc, k:k + 1], scalar2=modfm.t[:, slot_sh, k:k + 1],
                        op0=ALU.mult, op1=ALU.add), reads=[pt, modfm], writes=[dstT])
                else:
                    P.op('act', lambda e, k=k, kk=kk, pt=pt: e.activation(
                        out=dstT.t[:, k, tcol:tcol + 128], in_=pt.t[:, kk, :], func=AF.Identity,
                        scale=modfm.t[:, slot_sc, k:k + 1], bias=modfm.t[:, slot_sh, k:k + 1]),
                        reads=[pt, modfm], writes=[dstT])

    hT = P.carve("hT", R_H, [128, 16, 2048], BF16)
    for tt in range(16):
        xb = xt[tt % 2]
        P.dma('sp', xb.t, xw[tt * 128:(tt + 1) * 128, :], writes=[xb], dres=xb)
        norm_to_fm(xb.t, xb, hT, tt * 128, 0, 1, "a%d" % tt)
    for g in range(8, 24):
        ada_do(g)
    P.op('dve', lambda e: e.tensor_tensor(out=modfm.t[:, 3, :], in0=modfm.t[:, 3, :], in1=gffn.t, op=ALU.mult),
         reads=[modfm, gffn], writes=[modfm])

    ctx = dict(nc=nc, P=P, PB=PB, load_w=load_w, hT=hT, y=y, dbg_t=dbg_t, xw=xw, w_in=w_in, ident=ident,
               tribf=tribf, triuf=triuf, onesf=onesf, cos=cos, sin=sin, valid=valid, modfm=modfm, cw=cw, cb=cb,
               bg=bg, lam=lam, dan=dan, mln=mln, gtm=gtm, gtf=gtf, misc=misc, w_out=w_out, w_pq=w_pq,
               subkT=subkT, peer_u=peer_u, peer_v=peer_v, validrow_d=validrow_d, w_adaf=w_adaf, b_adaf=b_adaf,
               gfin=gfin, wring=wring, wstate=wstate, norm_to_fm=norm_to_fm, one1=one1, xt=xt, stage=stage, csbf=csbf, ps_tr=ps_tr)
    return ctx


def host_inputs(x, c, w_ada, b_ada, g_mix, w_in, conv_w, conv_b, b_igate, b_fgate,
                lambda_q1, lambda_k1, lambda_q2, lambda_k2, da_norm, ml_norm, w_out,
                g_ffn, w_pq, sub_keys, peer_u, peer_v, w_ada_final, b_ada_final, g_final):
    import ml_dtypes
    f32 = np.float32
    A = lambda a: np.ascontiguousarray(np.asarray(a, dtype=f32))
    x = A(x); c = A(c)

    def fm(v):
        return np.ascontiguousarray(A(v).reshape(16, 128).T)

    def rep(v, n=128):
        v = A(v).reshape(1, -1)
        return np.ascontiguousarray(np.broadcast_to(v, (n, v.shape[1])))

    half = 32
    inv = (10000.0 ** (-np.arange(half, dtype=np.float32) / half)).astype(np.float32)
    ident = np.eye(128, dtype=np.float32)
    triu = np.triu(np.ones((128, 128), dtype=np.float32))
    shared = {
        "w_ada": A(w_ada)[0], "b_ada": A(b_ada)[0].reshape(1, -1),
        "w_adaf": A(w_ada_final), "b_adaf": A(b_ada_final).reshape(1, -1), "gfin": A(g_final).reshape(1, -1),
        "gmix_fm": fm(np.asarray(g_mix)[0]), "gffn_fm": fm(np.asarray(g_ffn)[0]),
        "w_in": A(w_in)[0],
        "convw": np.ascontiguousarray(A(conv_w)[0].T.reshape(8, 128, 4).transpose(1, 0, 2)),
        "convb": np.ascontiguousarray(A(conv_b)[0].reshape(8, 128).T),
        "bgate": rep(np.concatenate([np.asarray(b_igate)[0], np.asarray(b_fgate)[0]])),
        "lam4": np.ascontiguousarray(np.stack([rep(np.asarray(v)[0]) for v in (lambda_q1, lambda_k1, lambda_q2, lambda_k2)], axis=1)),
        "danorm": rep(np.asarray(da_norm)[0]), "mlnorm": rep(np.asarray(ml_norm)[0]),
        "w_out": A(w_out)[0], "w_pq": A(w_pq)[0],
        "subkT": np.ascontiguousarray(A(sub_keys)[0].reshape(16, 128, 128).transpose(0, 2, 1)),
        "peer_u": A(peer_u)[0], "peer_v": A(peer_v)[0],
        "ident": ident.astype(ml_dtypes.bfloat16), "tribf": triu.astype(ml_dtypes.bfloat16), "triuf": triu,
    }
    maps = []
    for core in range(8):
        b, hf = core // 2, core % 2
        m = dict(shared)
        if hf == 1:
            xwin = x[b]
            pos = np.arange(S, dtype=np.float32)
            val = np.ones(S, dtype=np.float32)
        else:
            xwin = np.concatenate([np.zeros((1024, D), f32), x[b, :1024]], axis=0)
            pos = np.concatenate([np.zeros(1024, f32), np.arange(1024, dtype=f32)])
            val = np.concatenate([np.zeros(1024, f32), np.ones(1024, f32)])
        ang = pos[:, None] * inv[None, :]
        m["xw"] = np.ascontiguousarray(xwin)
        m["cfm"] = fm(c[b])
        m["cosT"] = np.ascontiguousarray(np.cos(ang).astype(f32).reshape(16, 128, 32).transpose(1, 0, 2))
        m["sinT"] = np.ascontiguousarray(np.sin(ang).astype(f32).reshape(16, 128, 32).transpose(1, 0, 2))
        m["valid"] = np.ascontiguousarray(val.reshape(16, 128).T)
        m["validrow"] = np.ascontiguousarray(np.broadcast_to(val[None, :], (128, S))).astype(ml_dtypes.bfloat16)
        maps.append(m)
    return maps


def phase_mixers(ctx):
    P, PB, load_w, hT, w_in = ctx['P'], ctx['PB'], ctx['load_w'], ctx['hT'], ctx['w_in']
    ident, tribf, cos, sin, valid, lam, dan, mln = (ctx[k] for k in ('ident', 'tribf', 'cos', 'sin', 'valid', 'lam', 'dan', 'mln'))
    ps_tr = ctx['ps_tr']
    mixT = P.carve("mixT", R_M, [128, 16, 1024], BF16)
    ctx['mixT'] = mixT
    sso = P.carve("sso", R_T2, [128, 4], F32)
    jk = P.carve("jk", R_T2 + 16, [128, 256], F32)
    B = R_T2 + 1056
    kT = P.carve("kT", B, [128, 2048], BF16)
    qT = P.carve("qT", B + 4096, [128, 1024], BF16)
    V = P.carve("V", B + 6144, [128, 16, 144], BF16)
    rot = [P.carve("rot%d" % i, B + 10752 + 256 * i, [128, 2, 2, 32], BF16) for i in range(2)]
    tA = P.carve("tA", B + 11264, [128, 2, 32], F32)
    tB = P.carve("tB", B + 11520, [128, 2, 32], F32)
    pt = [P.carve("pt%d" % i, B + 11776 + 1024 * i, [128, 4, 128], BF16) for i in range(2)]
    rz = P.carve("rz", B + 13824, [128, 4], F32)
    oa = P.carve("oa", B + 13888, [128, 128], F32)
    t2 = P.carve("t2", B + 14400, [128, 128], F32)
    oan = [P.carve("oan%d" % i, B + 14912 + 256 * i, [128, 128], BF16) for i in range(2)]
    ps_p = [PB("ps_p%d" % i, i, [128, 512]) for i in range(2)]
    ps_s = [PB("ps_s%d" % i, 2 + i, [128, 4, 128]) for i in range(2)]
    ps_o = [PB("ps_o%d" % i, 4 + i, [128, 2, 129]) for i in range(2)]
    cnt = {'p': 0, 's': 0, 'o': 0, 'r': 0, 't': 0, 'pt': 0, 'n': 0}

    def nxt(lst, key):
        r = lst[cnt[key] % len(lst)]
        cnt[key] += 1
        return r

    def proj_tok(w, c0, ncol, tt, psr, pcol):
        def mm(e):
            ins = None
            for kk in range(16):
                ins = e.matmul(psr.t[:, pcol:pcol + ncol], lhsT=hT.t[:, kk, tt * 128:(tt + 1) * 128],
                               rhs=w.t[:, kk, c0:c0 + ncol], start=(kk == 0), stop=(kk == 15))
            return ins
        P.op('pe', mm, reads=[hT, w], writes=[psr])

    def rope_T(psr, pcol, tt, dstT, dcol):
        src = psr.t[:, pcol:pcol + 128].rearrange("p (m h d) -> p m h d", m=2, h=2)
        cb_ = cos.t[:, tt:tt + 1, :].to_broadcast([128, 2, 32])
        sb_ = sin.t[:, tt:tt + 1, :].to_broadcast([128, 2, 32])
        r = nxt(rot, 'r')
        P.op('dve', lambda e: e.tensor_tensor(out=tA.t, in0=src[:, :, 0, :], in1=cb_, op=ALU.mult), reads=[psr, cos], writes=[tA])
        P.op('dve', lambda e: e.tensor_tensor(out=tB.t, in0=src[:, :, 1, :], in1=sb_, op=ALU.mult), reads=[psr, sin], writes=[tB])
        P.op('dve', lambda e: e.tensor_tensor(out=r.t[:, :, 0, :], in0=tA.t, in1=tB.t, op=ALU.subtract), reads=[tA, tB], writes=[r])
        P.op('dve', lambda e: e.tensor_tensor(out=tA.t, in0=src[:, :, 1, :], in1=cb_, op=ALU.mult), reads=[psr, cos, r], writes=[tA])
        P.op('dve', lambda e: e.tensor_tensor(out=tB.t, in0=src[:, :, 0, :], in1=sb_, op=ALU.mult), reads=[psr, sin, r], writes=[tB])
        P.op('dve', lambda e: e.tensor_tensor(out=r.t[:, :, 1, :], in0=tA.t, in1=tB.t, op=ALU.add), reads=[tA, tB], writes=[r])
        tr = nxt(ps_tr, 't')
        P.op('pe', lambda e: e.transpose(tr.t[:, 0, :], r.t.rearrange("p m h d -> p (m h d)"), ident.t), reads=[r, ident], writes=[tr])
        P.op('act', lambda e: e.copy(out=dstT.t[:, dcol:dcol + 128], in_=tr.t[:, 0, :]), reads=[tr], writes=[dstT])

    def rms_scale(src_ap, src_res, n, out_col):
        P.op('act', lambda e: e.activation(out=jk.t[:, 0:n], in_=src_ap, func=AF.Square, accum_out=sso.t[:, 0:1]),
             reads=[src_res], writes=[jk, sso])
        P.op('dve', lambda e: e.tensor_scalar(out=sso.t[:, 1:2], in0=sso.t[:, 0:1], scalar1=1.0 / n, scalar2=EPS,
                                              op0=ALU.mult, op1=ALU.add), reads=[sso], writes=[sso])
        P.op('act', lambda e: e.activation(out=sso.t[:, 2:3], in_=sso.t[:, 1:2], func=AF.Ln), reads=[sso], writes=[sso])
        P.op('act', lambda e: e.activation(out=sso.t[:, out_col:out_col + 1], in_=sso.t[:, 2:3], func=AF.Exp, scale=-0.5), reads=[sso], writes=[sso])

    P.op('dve', lambda e: e.tensor_copy(out=V.t[:, :, 128:129], in_=valid.t.unsqueeze(2)), reads=[valid], writes=[V])
    for grp in range(2):
        wq = load_w(w_in, OFF_DAQ + grp * 512, 512)
        wk = load_w(w_in, OFF_DAK + grp * 512, 512)
        wv = load_w(w_in, OFF_DAV + grp * 512, 512)
        for hl in range(4):
            hd = grp * 4 + hl
            c0 = hl * 128
            def post(pp, tt):
                rope_T(pp, 0, tt, kT, tt * 128)
                P.op('act', lambda e: e.activation(out=V.t[:, tt, 0:128], in_=pp.t[:, 128:256], func=AF.Identity,
                                                   scale=valid.t[:, tt:tt + 1]), reads=[pp, valid], writes=[V])
                if tt >= 8:
                    rope_T(pp, 256, tt, qT, (tt - 8) * 128)
            pend = None
            for tt in range(16):
                pp = nxt(ps_p, 'p')
                proj_tok(wk, c0, 128, tt, pp, 0)
                proj_tok(wv, c0, 128, tt, pp, 128)
                if tt >= 8:
                    proj_tok(wq, c0, 128, tt, pp, 256)
                if pend is not None:
                    post(*pend)
                pend = (pp, tt)
            post(*pend)
            if ctx.get('stop') == 'proj':
                ctx['kT'], ctx['qT'], ctx['V'] = kT, qT, V
                return
            groups = []
            for qi in range(8):
                nk = 9 + qi
                for m in range(2):
                    for g0 in range(0, nk, 4):
                        groups.append((qi, m, list(range(g0, min(g0 + 4, nk))), nk))
            gbuf = {}

            def emit_S(i):
                qi, m, ks, nk = groups[i]
                psb = nxt(ps_s, 's')
                ptb = nxt(pt, 'pt')
                gbuf[i] = (psb, ptb)

                def mm(e):
                    ins = None
                    for j, kj in enumerate(ks):
                        ins = e.matmul(psb.t[:, j, :], lhsT=kT.t[m * 64:(m + 1) * 64, kj * 128:(kj + 1) * 128],
                                       rhs=qT.t[m * 64:(m + 1) * 64, qi * 128:(qi + 1) * 128], start=True, stop=True)
                    return ins
                P.op('pe', mm, reads=[kT, qT], writes=[psb])
            emit_S(0)
            po = None
            for i in range(len(groups)):
                qi, m, ks, nk = groups[i]
                if m == 0 and ks[0] == 0:
                    po = nxt(ps_o, 'o')
                if i + 1 < len(groups):
                    emit_S(i + 1)
                psb, ptb = gbuf.pop(i)
                n = len(ks)
                P.op('act', lambda e, psb=psb, ptb=ptb, n=n: e.activation(out=ptb.t[:, 0:n, :], in_=psb.t[:, 0:n, :],
                                                                          func=AF.Exp, scale=0.125), reads=[psb], writes=[ptb])
                if (8 + qi) in ks:
                    j = ks.index(8 + qi)
                    P.op('dve', lambda e, ptb=ptb, j=j: e.tensor_tensor(out=ptb.t[:, j, :], in0=ptb.t[:, j, :], in1=tribf.t,
                                                                        op=ALU.mult), reads=[ptb, tribf], writes=[ptb])

                def av(e, ks=ks, ptb=ptb, m=m, po=po, nk=nk):
                    ins = None
                    for j, kj in enumerate(ks):
                        ins = e.matmul(po.t[:, m, :], lhsT=ptb.t[:, j, :], rhs=V.t[:, kj, 0:129],
                                       start=(kj == 0), stop=(kj == nk - 1))
                    return ins
                P.op('pe', av, reads=[ptb, V], writes=[po])
                if not (m == 1 and ks[-1] == nk - 1):
                    continue
                P.op('dve', lambda e, po=po: e.reciprocal(out=rz.t[:, 0:2], in_=po.t[:, :, 128]), reads=[po], writes=[rz])
                P.op('dve', lambda e: e.tensor_tensor(out=rz.t[:, 1:2], in0=rz.t[:, 1:2], in1=lam.t[:, 0:1], op=ALU.mult),
                     reads=[rz, lam], writes=[rz])
                P.op('act', lambda e, po=po: e.activation(out=t2.t, in_=po.t[:, 1, 0:128], func=AF.Identity, scale=rz.t[:, 1:2]),
                     reads=[po, rz], writes=[t2])
                P.op('dve', lambda e, po=po: e.scalar_tensor_tensor(out=oa.t, in0=po.t[:, 0, 0:128], scalar=rz.t[:, 0:1], in1=t2.t,
                                                                    op0=ALU.mult, op1=ALU.subtract), reads=[po, rz, t2], writes=[oa])
                rms_scale(oa.t, oa, 128, 3)
                on = nxt(oan, 'n')
                P.op('dve', lambda e, on=on: e.scalar_tensor_tensor(out=on.t, in0=oa.t, scalar=sso.t[:, 3:4], in1=dan.t,
                                                                    op0=ALU.mult, op1=ALU.mult), reads=[oa, sso, dan], writes=[on])
                tr = nxt(ps_tr, 't')
                P.op('pe', lambda e, tr=tr, on=on: e.transpose(tr.t[:, 0, :], on.t, ident.t), reads=[on, ident], writes=[tr])
                P.op('act', lambda e, tr=tr, hd=hd, qi=qi: e.copy(out=mixT.t[:, hd, qi * 128:(qi + 1) * 128], in_=tr.t[:, 0, :]),
                     reads=[tr], writes=[mixT])
    ctx['sso'] = sso
    ctx['jk'] = jk
    ctx['rms_scale'] = rms_scale
    ctx['nxt'] = nxt
    ctx['cnt'] = cnt
    ctx['ps_p'] = ps_p
    ctx['proj_tok'] = proj_tok


def phase_mlstm(ctx):
    import math
    P, PB, hT, w_in = ctx['P'], ctx['PB'], ctx['hT'], ctx['w_in']
    ident, tribf, triuf, onesf, valid, mln, cw, cb, bg = (ctx[k] for k in ('ident', 'tribf', 'triuf', 'onesf', 'valid', 'mln', 'cw', 'cb', 'bg'))
    ps_tr, mixT, nxt, sso, rms_scale = ctx['ps_tr'], ctx['mixT'], ctx['nxt'], ctx['sso'], ctx['rms_scale']
    wring, wstate = ctx['wring'], ctx['wstate']
    B = R_T2 + 1056
    G0 = B
    wg = P.carve("wg", G0, [128, 16, 8], BF16)
    graw = P.carve("graw", G0 + 256, [128, 16, 8], F32)
    gi = P.carve("gi", G0 + 768, [128, 16, 4], F32)
    gf = P.carve("gf", G0 + 1024, [128, 16, 4], F32)
    gA = P.carve("gA", G0 + 1280, [128, 16, 4], F32)
    gea = P.carve("gea", G0 + 1536, [128, 16, 4], F32)
    geft = P.carve("geft", G0 + 1792, [128, 16, 4], F32)
    gtmp = P.carve("gtmp", G0 + 2048, [128, 16, 4], F32)
    vm1 = P.carve("vm1", G0 + 2304, [128, 16], F32)
    B2 = G0 + 2368 + 8
    cur = {'o': B2}

    def AL(name, shape, dt):
        esz = 4 if dt == F32 else 2
        n = 1
        for d in shape[1:]:
            n *= d
        off = cur['o']
        cur['o'] = off + ((n * esz + 63) // 64) * 64
        return P.carve(name, off, shape, dt)
    pre = AL("pre", [128, 2052], BF16)
    acc = AL("acc", [128, 512], F32)
    qmT = AL("qmT", [128, 1024], BF16)
    kmT = AL("kmT", [128, 2048], BF16)
    Kt = AL("Kt", [128, 16, 128], BF16)
    Vp = [AL("Vp%d" % i, [128, 260], BF16) for i in range(2)]
    Cst = AL("Cst", [128, 257], F32)
    Cbf = AL("Cbf", [128, 258], BF16)
    Pt = [AL("Pt%d" % i, [128, 128], BF16) for i in range(2)]
    og = AL("og", [128, 256], F32)
    hh = AL("hh", [128, 256], F32)
    om = AL("om", [128, 256], BF16)
    dsc = AL("dsc", [128, 4], F32)
    vrow = AL("vrow", [128, 2048], BF16)
    assert cur['o'] <= ARENA, cur['o']
    P.dma('sp', vrow.t, ctx['validrow_d'], writes=[vrow], dres=vrow)

    ps_a = [PB("pm_a%d" % i, i, [128, 512]) for i in range(2)]
    ps_s = [PB("pm_s%d" % i, 2 + i, [128, 512]) for i in range(2)]
    ps_n = [PB("pm_n%d" % i, 4 + i, [128, 512]) for i in range(2)]
    c2 = {'a': 0, 's': 0, 'n': 0, 'vp': 0, 'pt': 0}

    def nx(lst, k):
        r = lst[c2[k] % len(lst)]
        c2[k] += 1
        return r

    P.dma('pool', wg.t, w_in[:, OFF_MLG:OFF_MLG + 8].rearrange("(k p) c -> p k c", p=128), writes=[wg], dres=wg)
    pg = nx(ps_a, 'a')

    def gmm(e):
        ins = None
        for tt in range(16):
            for kk in range(16):
                ins = e.matmul(pg.t[:, tt * 8:(tt + 1) * 8], lhsT=hT.t[:, kk, tt * 128:(tt + 1) * 128], rhs=wg.t[:, kk, :],
                               start=(kk == 0), stop=(kk == 15))
        return ins
    P.op('pe', gmm, reads=[hT, wg], writes=[pg])
    P.op('dve', lambda e: e.tensor_tensor(out=graw.t, in0=pg.t[:, 0:128].rearrange("p (t g) -> p t g", g=8),
                                          in1=bg.t.unsqueeze(1).to_broadcast([128, 16, 8]), op=ALU.add),
         reads=[pg, bg], writes=[graw])
    P.op('act', lambda e: e.activation(out=graw.t, in_=graw.t, func=AF.Tanh, scale=1.0 / 15.0), reads=[graw], writes=[graw])
    vb = valid.t.unsqueeze(2).to_broadcast([128, 16, 4])
    P.op('dve', lambda e: e.tensor_scalar(out=vm1.t, in0=valid.t, scalar1=-1.0, scalar2=1.0e4, op0=ALU.add, op1=ALU.mult),
         reads=[valid], writes=[vm1])
    P.op('dve', lambda e: e.scalar_tensor_tensor(out=gi.t, in0=graw.t[:, :, 0:4], scalar=15.0, in1=vb, op0=ALU.mult, op1=ALU.mult),
         reads=[graw, valid], writes=[gi])
    P.op('dve', lambda e: e.tensor_tensor(out=gi.t, in0=gi.t, in1=vm1.t.unsqueeze(2).to_broadcast([128, 16, 4]), op=ALU.add),
         reads=[gi, vm1], writes=[gi])
    P.op('act', lambda e: e.activation(out=gtmp.t, in_=graw.t[:, :, 4:8], func=AF.Exp, scale=-15.0), reads=[graw], writes=[gtmp])
    P.op('dve', lambda e: e.tensor_scalar(out=gtmp.t, in0=gtmp.t, scalar1=1.0, scalar2=None, op0=ALU.add), reads=[gtmp], writes=[gtmp])
    P.op('act', lambda e: e.activation(out=gtmp.t, in_=gtmp.t, func=AF.Ln), reads=[gtmp], writes=[gtmp])
    P.op('dve', lambda e: e.scalar_tensor_tensor(out=gf.t, in0=gtmp.t, scalar=-1.0, in1=vb, op0=ALU.mult, op1=ALU.mult),
         reads=[gtmp, valid], writes=[gf])
    pf = nx(ps_a, 'a')
    gf2 = gf.t.rearrange("p t g -> p (t g)")
    P.op('pe', lambda e: e.matmul(pf.t[:, 0:64], lhsT=triuf.t, rhs=gf2, start=True, stop=True), reads=[triuf, gf], writes=[pf])
    P.op('pe', lambda e: e.matmul(pf.t[:, 64:128], lhsT=onesf.t, rhs=gf2, start=True, stop=True), reads=[onesf, gf], writes=[pf])
    P.op('act', lambda e: e.activation(out=gA.t.rearrange("p t g -> p (t g)"), in_=pf.t[:, 0:64], func=AF.Exp), reads=[pf], writes=[gA])
    P.op('act', lambda e: e.activation(out=geft.t.rearrange("p t g -> p (t g)"), in_=pf.t[:, 64:128], func=AF.Exp), reads=[pf], writes=[geft])
    P.op('dve', lambda e: e.tensor_tensor(out=gtmp.t.rearrange("p t g -> p (t g)"), in0=gi.t.rearrange("p t g -> p (t g)"),
                                          in1=pf.t[:, 0:64], op=ALU.subtract), reads=[gi, pf, gtmp], writes=[gtmp])
    P.op('dve', lambda e: e.tensor_scalar(out=gtmp.t, in0=gtmp.t, scalar1=-0.5 * math.log(128.0), scalar2=None, op0=ALU.add),
         reads=[gtmp], writes=[gtmp])
    P.op('act', lambda e: e.activation(out=gea.t, in_=gtmp.t, func=AF.Exp), reads=[gtmp], writes=[gea])

    P.op('dve', lambda e: e.memset(pre.t[:, 0:4], 0.0), writes=[pre])

    def loadw2(pieces):
        r = wring[wstate['i'] % 3]
        wstate['i'] += 1

        def fn(e):
            return [e.dma_start(out=r.t[:, :, d0:d0 + n], in_=w_in[:, c0:c0 + n].rearrange("(k p) c -> p k c", p=128))
                    for (c0, n, d0) in pieces]
        P.dma('pool', None, None, writes=[r], dres=r, n=len(pieces), fn=fn)
        return r

    for hd in range(4):
        wA = loadw2([(OFF_MLQ + hd * 128, 128, 0), (OFF_MLK + hd * 128, 128, 128)])
        wB = loadw2([(OFF_MLV + hd * 256, 256, 0), (OFF_MLO + hd * 256, 256, 256)])
        for which in range(2):
            ch = which * 4 + hd
            for tc in range(4):
                pa = nx(ps_a, 'a')

                def mm(e, pa=pa, which=which, tc=tc, wA=wA):
                    ins = None
                    for kk in range(16):
                        ins = e.matmul(pa.t, lhsT=wA.t[:, kk, which * 128:(which + 1) * 128], rhs=hT.t[:, kk, tc * 512:(tc + 1) * 512],
                                       start=(kk == 0), stop=(kk == 15))
                    return ins
                P.op('pe', mm, reads=[wA, hT], writes=[pa])
                P.op('dve', lambda e, pa=pa, tc=tc: e.tensor_tensor(out=pre.t[:, 4 + tc * 512:4 + (tc + 1) * 512], in0=pa.t,
                                                                    in1=vrow.t[:, tc * 512:(tc + 1) * 512], op=ALU.mult),
                     reads=[pa, vrow], writes=[pre])
            for tc in range(4):
                if which == 0 and tc < 2:
                    continue
                b0 = 4 + tc * 512 - 3
                P.op('dve', lambda e, b0=b0, ch=ch: e.tensor_scalar(out=acc.t, in0=pre.t[:, b0:b0 + 512], scalar1=cw.t[:, ch, 0:1],
                                                                    scalar2=None, op0=ALU.mult), reads=[pre, cw], writes=[acc])
                for w_ in range(1, 4):
                    P.op('dve', lambda e, b0=b0, ch=ch, w_=w_: e.scalar_tensor_tensor(
                        out=acc.t, in0=pre.t[:, b0 + w_:b0 + w_ + 512], scalar=cw.t[:, ch, w_:w_ + 1], in1=acc.t,
                        op0=ALU.mult, op1=ALU.add), reads=[pre, cw, acc], writes=[acc])
                if which == 0:
                    dst, dres = qmT.t[:, (tc - 2) * 512:(tc - 1) * 512], qmT
                else:
                    dst, dres = kmT.t[:, tc * 512:(tc + 1) * 512], kmT
                P.op('act', lambda e, dst=dst, ch=ch: e.activation(out=dst, in_=acc.t, func=AF.Silu, bias=cb.t[:, ch:ch + 1]),
                     reads=[acc, cb], writes=[dres])
        for tt in range(16):
            tr = nxt(ps_tr, 't')
            P.op('pe', lambda e, tr=tr, tt=tt: e.transpose(tr.t[:, 0, :], kmT.t[:, tt * 128:(tt + 1) * 128], ident.t),
                 reads=[kmT, ident], writes=[tr])
            P.op('act', lambda e, tr=tr, tt=tt: e.copy(out=Kt.t[:, tt, :], in_=tr.t[:, 0, :]), reads=[tr], writes=[Kt])
        P.op('dve', lambda e: e.memset(Cst.t, 0.0), writes=[Cst])
        P.op('dve', lambda e: e.memset(Cbf.t, 0.0), writes=[Cbf])
        for tt in range(16):
            col = tt * 4 + hd
            own = tt >= 8
            qi = tt - 8
            vp = nx(Vp, 'vp')
            pa = nx(ps_a, 'a')

            def vmm(e, pa=pa, tt=tt, wB=wB, own=own):
                ins = None
                n = 512 if own else 256
                for kk in range(16):
                    ins = e.matmul(pa.t[:, 0:n], lhsT=hT.t[:, kk, tt * 128:(tt + 1) * 128], rhs=wB.t[:, kk, 0:n],
                                   start=(kk == 0), stop=(kk == 15))
                return ins
            P.op('pe', vmm, reads=[hT, wB], writes=[pa])
            ea_ap = gea.t[:, tt, hd:hd + 1]
            P.op('act', lambda e, pa=pa, vp=vp, ea_ap=ea_ap: e.activation(out=vp.t[:, 0:256], in_=pa.t[:, 0:256], func=AF.Identity, scale=ea_ap),
                 reads=[pa, gea], writes=[vp])
            P.op('dve', lambda e, vp=vp, ea_ap=ea_ap: e.tensor_copy(out=vp.t[:, 256:257], in_=ea_ap), reads=[gea], writes=[vp])
            if own:
                P.op('act', lambda e, pa=pa: e.activation(out=og.t, in_=pa.t[:, 256:512], func=AF.Exp, scale=-1.0), reads=[pa], writes=[og])
                P.op('dve', lambda e: e.tensor_scalar(out=og.t, in0=og.t, scalar1=1.0, scalar2=None, op0=ALU.add), reads=[og], writes=[og])
                P.op('dve', lambda e: e.reciprocal(out=og.t, in_=og.t), reads=[og], writes=[og])
                psb = nx(ps_s, 's')
                P.op('pe', lambda e, psb=psb, tt=tt, qi=qi: e.matmul(psb.t[:, 0:128], lhsT=kmT.t[:, tt * 128:(tt + 1) * 128],
                                                                     rhs=qmT.t[:, qi * 128:(qi + 1) * 128], start=True, stop=True),
                     reads=[kmT, qmT], writes=[psb])
                ptb = nx(Pt, 'pt')
                P.op('dve', lambda e, psb=psb, ptb=ptb: e.tensor_tensor(out=ptb.t, in0=psb.t[:, 0:128], in1=tribf.t, op=ALU.mult),
                     reads=[psb, tribf], writes=[ptb])
                pn = nx(ps_n, 'n')

                def nmm(e, pn=pn, qi=qi, ptb=ptb, vp=vp):
                    e.matmul(pn.t[:, 0:257], lhsT=qmT.t[:, qi * 128:(qi + 1) * 128], rhs=Cbf.t[:, 0:257], start=True, stop=False)
                    return e.matmul(pn.t[:, 0:257], lhsT=ptb.t, rhs=vp.t[:, 0:257], start=False, stop=True)
                P.op('pe', nmm, reads=[qmT, Cbf, ptb, vp], writes=[pn])
                A_ap = gA.t[:, tt, hd:hd + 1]
                P.op('dve', lambda e, pn=pn: e.tensor_scalar(out=dsc.t[:, 2:3], in0=pn.t[:, 256:257], scalar1=-1.0, scalar2=None,
                                                             op0=ALU.mult), reads=[pn], writes=[dsc])
                P.op('dve', lambda e, pn=pn: e.tensor_tensor(out=dsc.t[:, 2:3], in0=dsc.t[:, 2:3], in1=pn.t[:, 256:257], op=ALU.max),
                     reads=[pn, dsc], writes=[dsc])
                P.op('dve', lambda e, A_ap=A_ap: e.tensor_tensor(out=dsc.t[:, 0:1], in0=dsc.t[:, 2:3], in1=A_ap, op=ALU.mult),
                     reads=[dsc, gA], writes=[dsc])
                P.op('dve', lambda e: e.tensor_scalar(out=dsc.t[:, 1:2], in0=dsc.t[:, 0:1], scalar1=1.0, scalar2=None, op0=ALU.max),
                     reads=[dsc], writes=[dsc])
                P.op('dve', lambda e: e.reciprocal(out=dsc.t[:, 2:3], in_=dsc.t[:, 1:2]), reads=[dsc], writes=[dsc])
                P.op('dve', lambda e, A_ap=A_ap: e.tensor_tensor(out=dsc.t[:, 3:4], in0=dsc.t[:, 2:3], in1=A_ap, op=ALU.mult),
                     reads=[dsc, gA], writes=[dsc])
                P.op('act', lambda e, pn=pn: e.activation(out=hh.t, in_=pn.t[:, 0:256], func=AF.Identity, scale=dsc.t[:, 3:4]),
                     reads=[pn, dsc], writes=[hh])
                rms_scale(hh.t, hh, 256, 3)
                P.op('dve', lambda e: e.scalar_tensor_tensor(out=hh.t, in0=hh.t, scalar=sso.t[:, 3:4], in1=mln.t, op0=ALU.mult, op1=ALU.mult),
                     reads=[hh, sso, mln], writes=[hh])
                P.op('dve', lambda e: e.tensor_tensor(out=om.t, in0=hh.t, in1=og.t, op=ALU.mult), reads=[hh, og], writes=[om])
                for half in range(2):
                    tr = nxt(ps_tr, 't')
                    P.op('pe', lambda e, tr=tr, half=half: e.transpose(tr.t[:, 0, :], om.t[:, half * 128:(half + 1) * 128], ident.t),
                         reads=[om, ident], writes=[tr])
                    P.op('act', lambda e, tr=tr, half=half, hd=hd, qi=qi: e.copy(
                        out=mixT.t[:, 8 + hd * 2 + half, qi * 128:(qi + 1) * 128], in_=tr.t[:, 0, :]), reads=[tr], writes=[mixT])
            if tt < 15:
                pu = nx(ps_n, 'n')
                P.op('pe', lambda e, pu=pu, tt=tt, vp=vp: e.matmul(pu.t[:, 0:257], lhsT=Kt.t[:, tt, :], rhs=vp.t[:, 0:257], start=True, stop=True),
                     reads=[Kt, vp], writes=[pu])
                P.op('dve', lambda e, pu=pu: e.tensor_tensor(out=Cst.t, in0=Cst.t, in1=pu.t[:, 0:257], op=ALU.add), reads=[Cst, pu], writes=[Cst])
                P.op('dve', lambda e, tt=tt, hd=hd: e.tensor_scalar(out=Cst.t, in0=Cst.t, scalar1=geft.t[:, tt, hd:hd + 1], scalar2=None, op0=ALU.mult),
                     reads=[Cst, geft], writes=[Cst])
                P.op('act', lambda e: e.copy(out=Cbf.t[:, 0:257], in_=Cst.t), reads=[Cst], writes=[Cbf])


def phase_wout_norm2(ctx):
    P, PB, load_w = ctx['P'], ctx['PB'], ctx['load_w']
    mixT, gtm, xw = ctx['mixT'], ctx['gtm'], ctx['xw']
    x1 = P.carve("x1", R_H, [128, 8, 2048], F32)
    ctx['x1'] = x1
    xs = [P.carve("xs%d" % i, R_T2 + 2048 * i, [128, 512], F32) for i in range(4)]
    ps = [PB("pw%d" % i, i, [128, 512]) for i in range(4)]
    n = 0
    for cg in range(4):
        w = load_w(ctx['w_out'], cg * 512, 512)
        for qi in range(8):
            pp = ps[n % 4]
            xb = xs[n % 4]
            n += 1
            P.dma('sp', xb.t, xw[(8 + qi) * 128:(9 + qi) * 128, cg * 512:(cg + 1) * 512], writes=[xb], dres=xb)

            def mm(e, pp=pp, qi=qi, w=w):
                ins = None
                for kk in range(16):
                    ins = e.matmul(pp.t, lhsT=mixT.t[:, kk, qi * 128:(qi + 1) * 128], rhs=w.t[:, kk, :], start=(kk == 0), stop=(kk == 15))
                return ins
            P.op('pe', mm, reads=[mixT, w], writes=[pp])
            dst = x1.t[:, qi, cg * 512:(cg + 1) * 512]
            P.op('dve', lambda e, pp=pp, dst=dst, cg=cg: e.tensor_tensor(out=dst, in0=pp.t, in1=gtm.t[:, cg * 512:(cg + 1) * 512], op=ALU.mult),
                 reads=[pp, gtm], writes=[x1])
            P.op('dve', lambda e, dst=dst, xb=xb: e.tensor_tensor(out=dst, in0=dst, in1=xb.t, op=ALU.add), reads=[x1, xb], writes=[x1])
    h2T = P.carve("h2T", R_M, [128, 16, 1024], BF16)
    ctx['h2T'] = h2T
    for qi in range(8):
        ctx['norm_to_fm'](x1.t[:, qi, :], x1, h2T, qi * 128, 2, 3, "n2")


def phase_peer(ctx):
    P, PB = ctx['P'], ctx['PB']
    h2T, x1, ident, gtf, ps_tr = ctx['h2T'], ctx['x1'], ctx['ident'], ctx['gtf'], ctx['ps_tr']
    peer_u, peer_v, w_pq, subkT = ctx['peer_u'], ctx['peer_v'], ctx['w_pq'], ctx['subkT']
    Z1, Z2, Z3 = R_W, R_C + 256, R_T
    sA = P.carve("sA", Z1, [128, 8, 8, 128], BF16)
    sB = P.carve("sB", Z1 + 16384, [128, 8, 8, 128], BF16)

    def diag(idx):
        return (dg['3'], dg['3'].t[:, idx, :]) if idx < 36 else (dg['2'], dg['2'].t[:, idx - 36, :])
    dg = {}

    wpq = [P.carve("wpq%d" % i, Z3 + 16384 * i, [128, 16, 512], BF16) for i in range(2)]
    skT = P.carve("skT", Z2, [128, 16, 128], BF16)
    sf = P.carve("sf", Z1 + 32768, [128, 8, 2, 128], F32)
    qpT = [P.carve("qpT%d" % i, Z1 + 32768 + 8192 + 2048 * i, [128, 1024], BF16) for i in range(2)]
    tk = P.carve("tk", Z1 + 32768 + 12288, [128, 1024], F32)
    ck = P.carve("ck", Z2 + 4096, [128, 8, 8], F32)
    eck = P.carve("eck", Z2 + 4096 + 256, [128, 8, 8], F32)
    P.dma('pool', skT.t, subkT.rearrange("j d n -> d j n"), writes=[skT], dres=skT)
    psq = [PB("psq%d" % i, i, [128, 512]) for i in range(2)]
    pss = [PB("pss%d" % i, 2 + i, [128, 512]) for i in range(2)]
    n_q = 0
    t1, t2, t3 = tk.t[:, 0:16], tk.t[:, 16:32], tk.t[:, 32:48]
    cand = tk.t[:, 64:320]
    tmp = tk.t[:, 320:448]
    cand2 = tk.t[:, 320:576]
    T3 = tk.t[:, 640:768].rearrange("p (a b) -> p a b", a=8)
    E3 = tk.t[:, 768:896].rearrange("p (a b) -> p a b", a=8)
    Zv = tk.t[:, 896:904]
    C1 = tk.t[:, 904:912]
    NEG = -1.0e30
    for cg in range(4):
        w = wpq[cg % 2]
        P.dma('pool', w.t, w_pq[:, cg * 512:(cg + 1) * 512].rearrange("(k p) c -> p k c", p=128), writes=[w], dres=w)
        for hl in range(2):
            h = cg * 2 + hl
            for c in range(2):
                jj = hl * 2 + c
                j = cg * 4 + jj
                qb = qpT[j % 2]
                for th in range(2):
                    pq = psq[n_q % 2]
                    n_q += 1

                    def mm(e, pq=pq, w=w, jj=jj, th=th):
                        ins = None
                        for kk in range(16):
                            ins = e.matmul(pq.t, lhsT=w.t[:, kk, jj * 128:(jj + 1) * 128], rhs=h2T.t[:, kk, th * 512:(th + 1) * 512],
                                           start=(kk == 0), stop=(kk == 15))
                        return ins
                    P.op('pe', mm, reads=[w, h2T], writes=[pq])
                    P.op('act', lambda e, pq=pq, qb=qb, th=th: e.copy(out=qb.t[:, th * 512:(th + 1) * 512], in_=pq.t), reads=[pq], writes=[qb])
                for tq in range(2):
                    pz = pss[(j * 2 + tq) % 2]

                    def mm2(e, pz=pz, qb=qb, tq=tq, j=j):
                        ins = None
                        for t4 in range(4):
                            tt = tq * 4 + t4
                            ins = e.matmul(pz.t[:, t4 * 128:(t4 + 1) * 128], lhsT=qb.t[:, tt * 128:(tt + 1) * 128], rhs=skT.t[:, j, :],
                                           start=True, stop=True)
                        return ins
                    P.op('pe', mm2, reads=[qb, skT], writes=[pz])
                    P.op('act', lambda e, pz=pz, tq=tq, c=c: e.copy(out=sf.t[:, tq * 4:(tq + 1) * 4, c, :],
                                                                   in_=pz.t.rearrange("p (t n) -> p t n", t=4)), reads=[pz], writes=[sf])
            for tt in range(8):
                for c, tdst in ((0, t1), (1, t2)):
                    src = sf.t[:, tt, c, :]
                    P.op('dve', lambda e, src=src, tdst=tdst: e.max(out=tdst[:, 0:8], in_=src), reads=[sf], writes=[tk])
                    P.op('dve', lambda e, src=src, tdst=tdst: e.match_replace(out=tmp, in_to_replace=tdst[:, 0:8], in_values=src, imm_value=NEG),
                         reads=[sf, tk], writes=[tk])
                    P.op('dve', lambda e, tdst=tdst: e.max(out=tdst[:, 8:16], in_=tmp), reads=[tk], writes=[tk])
                P.op('dve', lambda e: e.tensor_tensor(out=cand.rearrange("p (a b) -> p a b", a=16), in0=t1.unsqueeze(2).to_broadcast([128, 16, 16]),
                                                      in1=t2.unsqueeze(1).to_broadcast([128, 16, 16]), op=ALU.add), reads=[tk], writes=[tk])
                P.op('dve', lambda e: e.max(out=t3[:, 0:8], in_=cand), reads=[tk], writes=[tk])
                P.op('dve', lambda e: e.match_replace(out=cand2, in_to_replace=t3[:, 0:8], in_values=cand, imm_value=NEG), reads=[tk], writes=[tk])
                P.op('dve', lambda e: e.max(out=t3[:, 8:16], in_=cand2), reads=[tk], writes=[tk])
                P.op('dve', lambda e, tt=tt: e.tensor_copy(out=T3[:, tt, :], in_=t3), reads=[tk], writes=[tk])
            P.op('dve', lambda e: e.tensor_tensor(out=E3, in0=T3, in1=T3[:, :, 0:1].to_broadcast([128, 8, 16]), op=ALU.subtract),
                 reads=[tk], writes=[tk])
            P.op('act', lambda e: e.activation(out=E3, in_=E3, func=AF.Exp), reads=[tk], writes=[tk])
            P.op('dve', lambda e: e.tensor_reduce(out=Zv, in_=E3, axis=AX.X, op=ALU.add), reads=[tk], writes=[tk])
            P.op('act', lambda e: e.activation(out=Zv, in_=Zv, func=AF.Ln), reads=[tk], writes=[tk])
            P.op('dve', lambda e: e.tensor_tensor(out=C1, in0=T3[:, :, 15], in1=T3[:, :, 0], op=ALU.subtract), reads=[tk], writes=[tk])
            P.op('dve', lambda e, h=h: e.tensor_tensor(out=ck.t[:, :, h], in0=C1, in1=Zv, op=ALU.subtract), reads=[tk], writes=[ck])
            P.op('dve', lambda e, h=h: e.tensor_tensor(out=sA.t[:, :, h, :], in0=sf.t[:, :, 0, :],
                                                       in1=T3[:, :, 15:16].to_broadcast([128, 8, 128]), op=ALU.subtract),
                 reads=[sf, tk], writes=[sA])
            P.op('act', lambda e, h=h: e.copy(out=sB.t[:, :, h, :], in_=sf.t[:, :, 1, :]), reads=[sf], writes=[sB])
    P.op('act', lambda e: e.activation(out=eck.t, in_=ck.t, func=AF.Exp), reads=[ck], writes=[eck])
    dg['3'] = P.carve("dg3", Z3 + 30720, [128, 36, 128], BF16)
    dg['2'] = P.carve("dg2", Z2 + 8192, [128, 28, 128], BF16)
    assert Z3 + 30720 + 36 * 256 <= ARENA and Z2 + 8192 + 28 * 256 <= R_C + C_GTF
    for idx in range(64):
        tt, h = idx // 8, idx % 8
        dres, dap = diag(idx)
        P.op('dve', lambda e, dap=dap, tt=tt, h=h: e.tensor_scalar(out=dap, in0=ident.t, scalar1=eck.t[:, tt, h:h + 1], scalar2=None, op0=ALU.mult),
             reads=[ident, eck], writes=[dres])
    vring = [P.carve("vr%d" % i, Z1 + 32768 + 4096 * i, [128, 2048], BF16) for i in range(4)]
    vring.append(P.carve("vr4", Z2 + 4096, [128, 2048], BF16))
    vring.append(P.carve("vr5", Z3 + 4096, [128, 2048], BF16))
    ubuf = P.carve("ubuf", Z2, [128, 2048], BF16)
    uTb = P.carve("uTb", Z3, [128, 16, 128], BF16)
    biasc = [P.carve("biasc%d" % i, Z2 + 15360 + 256 * i, [128, 8, 8], F32) for i in range(2)]
    NEG_ = 4
    EG = [P.carve("EG%d" % i, Z3 + 8192 + 2048 * i, [128, 8, 128], BF16) for i in range(NEG_)]
    WT = [P.carve("WT%d" % i, Z3 + 16384 + 2048 * i, [128, 1024], BF16) for i in range(5)]
    gelb = [P.carve("gel%d" % i, Z3 + 26624 + 2048 * i, [128, 1024], BF16) for i in range(2)]
    NV = 6
    NW = 5

    psA = [PB("psA%d" % i, i, [128, 512]) for i in range(2)]
    psG = [PB("psG%d" % i, 2 + i, [128, 512]) for i in range(2)]
    psY = [PB("psY%d" % i, 4 + i, [128, 512]) for i in range(2)]
    n_y = {'i': 0}

    def emit_load_u(ec):
        P.dma('pool', ubuf.t, peer_u[ec * 128:(ec + 1) * 128, :], writes=[ubuf], dres=ubuf)

    def emit_load_v(ec):
        vb = vring[ec % NV]
        P.dma('pool', vb.t, peer_v[ec * 128:(ec + 1) * 128, :], writes=[vb], dres=vb)

    def emit_vscale(ec):
        vb = vring[ec % NV]
        P.op('dve', lambda e, vb=vb: e.tensor_tensor(out=vb.t, in0=vb.t, in1=gtf.t, op=ALU.mult), reads=[vb, gtf], writes=[vb])

    def items_stageA(ec):
        items = []
        for half in range(2):
            def it(half=half):
                tr = ps_tr[half]

                def trf(e):
                    ins = None
                    for kk in range(8):
                        k = half * 8 + kk
                        ins = e.transpose(tr.t[:, kk, :], ubuf.t[:, k * 128:(k + 1) * 128], ident.t)
                    return ins
                P.op('pe', trf, reads=[ubuf, ident], writes=[tr])
                P.op('dve', lambda e: e.tensor_copy(out=uTb.t[:, half * 8:(half + 1) * 8, :], in_=tr.t), reads=[tr], writes=[uTb])
                if half == 1 and ec + 1 < 128:
                    emit_load_u(ec + 1)
            items.append(it)
        for th in range(2):
            for part in range(2):
                def it(th=th, part=part):
                    pa = psA[th]

                    def amm(e):
                        ins = None
                        for k in range(part * 8, part * 8 + 8):
                            ins = e.matmul(pa.t, lhsT=uTb.t[:, k, :], rhs=h2T.t[:, k, th * 512:(th + 1) * 512], start=(k == 0), stop=(k == 15))
                        return ins
                    P.op('pe', amm, reads=[uTb, h2T], writes=[pa])
                items.append(it)
        items.append(lambda: emit_gelu(ec))
        return items

    def emit_gelu(ec):
        gel = gelb[ec % 2]
        for th in range(2):
            P.op('act', lambda e, th=th: e.activation(out=gel.t[:, th * 512:(th + 1) * 512], in_=psA[th].t, func=AF.Gelu),
                 reads=[psA[th]], writes=[gel])

    def items_y(ec0):
        pair = [(WT[ec0 % NW], vring[ec0 % NV]), (WT[(ec0 + 1) % NW], vring[(ec0 + 1) % NV])]
        items = []
        for tt in range(8):
            for cb_ in range(4):
                def it(tt=tt, cb_=cb_):
                    py = psY[n_y['i'] % 2]
                    n_y['i'] += 1

                    def ymm(e):
                        ins = None
                        for i, (w_, v_) in enumerate(pair):
                            ins = e.matmul(py.t, lhsT=w_.t[:, tt * 128:(tt + 1) * 128], rhs=v_.t[:, cb_ * 512:(cb_ + 1) * 512],
                                           start=(i == 0), stop=(i == 1))
                        return ins
                    P.op('pe', ymm, reads=[pair[0][0], pair[0][1], pair[1][0], pair[1][1]], writes=[py])
                    dst = x1.t[:, tt, cb_ * 512:(cb_ + 1) * 512]
                    P.op('dve', lambda e: e.tensor_tensor(out=dst, in0=dst, in1=py.t, op=ALU.add), reads=[x1, py], writes=[x1])
                items.append(it)
        return items

    def emit_bias(ec):
        b_ = biasc[ec % 2]
        P.op('dve', lambda e: e.tensor_copy(out=b_.t, in_=sA.t[:, :, :, ec]), reads=[sA], writes=[b_])

    def gate_elem(g):
        ec, tt = g // 8, g % 8
        eg_ = EG[g % NEG_]
        b_ = biasc[ec % 2]

        def ex(e):
            ins = None
            for h in range(8):
                ins = e.activation(out=eg_.t[:, h, :], in_=sB.t[:, tt, h, :], func=AF.Exp, bias=b_.t[:, tt, h:h + 1])
            return ins
        P.op('act', ex, reads=[sB, b_], writes=[eg_])
        P.op('dve', lambda e: e.scalar_tensor_tensor(out=eg_.t, in0=eg_.t, scalar=1.0, in1=eg_.t, op0=ALU.is_ge, op1=ALU.mult),
             reads=[eg_], writes=[eg_])

    def gate_pe(g):
        ec, tt = g // 8, g % 8
        g_ = EG[g % NEG_]
        pg = psG[tt // 4]
        rds = [g_]
        daps = []
        for h in range(8):
            dres, dap = diag(tt * 8 + h)
            daps.append(dap)
            if dres not in rds:
                rds.append(dres)

        def gmm(e):
            ins = None
            for h in range(8):
                ins = e.matmul(pg.t[:, (tt % 4) * 128:(tt % 4 + 1) * 128], lhsT=g_.t[:, h, :], rhs=daps[h], start=(h == 0), stop=(h == 7))
            return ins
        P.op('pe', gmm, reads=rds, writes=[pg])

    def emit_WT_half(ec, th):
        wt = WT[ec % NW]
        gel = gelb[ec % 2]
        P.op('dve', lambda e: e.tensor_tensor(out=wt.t[:, th * 512:(th + 1) * 512], in0=psG[th].t,
                                              in1=gel.t[:, th * 512:(th + 1) * 512], op=ALU.mult),
             reads=[psG[th], gel], writes=[wt])

    emit_load_u(0)
    emit_load_v(0)
    emit_vscale(0)
    for it in items_stageA(0):
        it()
    emit_bias(0)
    emit_bias(1)
    gate_elem(0)
    gate_elem(1)
    gate_elem(2)
    ypend = []
    for ec in range(128):
        if ec + 1 < 128:
            emit_load_v(ec + 1)
        SA = items_stageA(ec + 1) if ec + 1 < 128 else []
        if ec % 2 == 0 and ec >= 2:
            ypend = items_y(ec - 2)
        ny = min(16, len(ypend))
        YL = ypend[:ny]
        ypend = ypend[ny:]
        plan = [[] for _ in range(8)]
        for k, it in enumerate(SA):
            plan[min(7, k + 1)].append(it)
        yq = list(YL)
        quota = [3, 2, 2, 2, 2, 2, 2, 1]
        for tt in range(8):
            for _ in range(quota[tt]):
                if yq:
                    plan[tt].append(yq.pop(0))
        plan[7].extend(yq)
        for tt in range(8):
            g = ec * 8 + tt
            gate_pe(g)
            if g + 3 < 1024:
                gate_elem(g + 3)
            if tt == 4:
                emit_WT_half(ec, 0)
            for it in plan[tt]:
                it()
        emit_WT_half(ec, 1)
        if ec + 2 < 128:
            emit_bias(ec + 2)
        if ec + 1 < 128:
            emit_vscale(ec + 1)
    for it in ypend:
        it()
    for it in items_y(126):
        it()


def phase_final(ctx):
    P, PB, x1, csbf, y = ctx['P'], ctx['PB'], ctx['x1'], ctx['csbf'], ctx['y']
    w_adaf, b_adaf, gfin = ctx['w_adaf'], ctx['b_adaf'], ctx['gfin']
    Z1 = R_W
    wb = [P.carve("wf%d" % i, Z1 + 16384 * i, [128, 16, 512], BF16) for i in range(2)]
    sho = P.carve("sho", Z1 + 32768, [128, 2048], F32)
    gsc = P.carve("gsc", Z1 + 40960, [128, 2048], F32)
    Z3 = R_T
    brow = P.carve("fbrow", Z3, [1, 512], F32)
    mrow = P.carve("fmrow", Z3 + 2048, [1, 512], F32)
    grow = P.carve("fgrow", Z3 + 4096, [1, 2048], F32)
    one1 = P.carve("fone1", Z3 + 12288, [1, 128], F32)
    fss = P.carve("fss", Z3 + 12800, [128, 4], F32)
    fjk = P.carve("fjk", Z3 + 12816, [128, 2048], BF16)
    ot = [P.carve("ot%d" % i, Z3 + 16912 + 8192 * i, [128, 2048], F32) for i in range(2)]
    ps_row = PB("f_row", 0, [1, 512])
    ps_bc = PB("f_bc", 1, [128, 512])
    P.op('dve', lambda e: e.memset(one1.t, 1.0), writes=[one1])
    P.dma('sp', grow.t, gfin, writes=[grow], dres=grow)
    for g in range(8):
        w = wb[g % 2]
        P.dma('pool', w.t, w_adaf[:, g * 512:(g + 1) * 512].rearrange("(k p) c -> p k c", p=128), writes=[w], dres=w)
        P.dma('sp', brow.t, b_adaf[:, g * 512:(g + 1) * 512], writes=[brow], dres=brow)

        def mm(e, w=w):
            ins = None
            for k in range(16):
                ins = e.matmul(ps_row.t, lhsT=csbf.t[:, k:k + 1], rhs=w.t[:, k, :], start=(k == 0), stop=(k == 15))
            return ins
        P.op('pe', mm, reads=[csbf, w], writes=[ps_row])
        sub = g % 4
        if g < 4:
            P.op('dve', lambda e: e.tensor_tensor(out=mrow.t, in0=ps_row.t, in1=brow.t, op=ALU.add), reads=[ps_row, brow], writes=[mrow])
            dst = sho.t[:, sub * 512:(sub + 1) * 512]
            dres = sho
        else:
            P.op('dve', lambda e: e.scalar_tensor_tensor(out=mrow.t, in0=ps_row.t, scalar=1.0, in1=brow.t, op0=ALU.add, op1=ALU.add),
                 reads=[ps_row, brow], writes=[mrow])
            P.op('dve', lambda e, sub=sub: e.tensor_tensor(out=mrow.t, in0=mrow.t, in1=grow.t[0:1, sub * 512:(sub + 1) * 512], op=ALU.mult),
                 reads=[mrow, grow], writes=[mrow])
            dst = gsc.t[:, sub * 512:(sub + 1) * 512]
            dres = gsc
        P.op('pe', lambda e: e.matmul(ps_bc.t, lhsT=one1.t[0:1, 0:128], rhs=mrow.t[0:1, :], start=True, stop=True), reads=[mrow, one1], writes=[ps_bc])
        P.op('act', lambda e, dst=dst: e.copy(out=dst, in_=ps_bc.t), reads=[ps_bc], writes=[dres])
    for tt in range(8):
        src = x1.t[:, tt, :]
        o = ot[tt % 2]
        P.op('act', lambda e, src=src: e.activation(out=fjk.t, in_=src, func=AF.Square, accum_out=fss.t[:, 0:1]), reads=[x1], writes=[fjk, fss])
        P.op('dve', lambda e: e.tensor_scalar(out=fss.t[:, 1:2], in0=fss.t[:, 0:1], scalar1=1.0 / D, scalar2=EPS, op0=ALU.mult, op1=ALU.add),
             reads=[fss], writes=[fss])
        P.op('act', lambda e: e.activation(out=fss.t[:, 3:4], in_=fss.t[:, 1:2], func=AF.Sqrt), reads=[fss], writes=[fss])
        P.op('dve', lambda e: e.reciprocal(out=fss.t[:, 2:3], in_=fss.t[:, 3:4]), reads=[fss], writes=[fss])
        P.op('dve', lambda e, src=src, o=o: e.scalar_tensor_tensor(out=o.t, in0=src, scalar=fss.t[:, 2:3], in1=gsc.t, op0=ALU.mult, op1=ALU.mult),
             reads=[x1, fss, gsc], writes=[o])
        P.op('dve', lambda e, o=o: e.tensor_tensor(out=o.t, in0=o.t, in1=sho.t, op=ALU.add), reads=[o, sho], writes=[o])
        P.dma('sp', y[tt * 128:(tt + 1) * 128, :], o.t, reads=[o], dres=o, is_out=True)


_CACHE = {}


def kernel(**inputs):
    maps = host_inputs(**inputs)
    if 'nc' not in _CACHE:
        ctx = build_program()
        phase_mixers(ctx)
        phase_mlstm(ctx)
        phase_wout_norm2(ctx)
        phase_peer(ctx)
        phase_final(ctx)
        ctx['P'].finish()
        _CACHE['nc'] = ctx['nc']
    res = run_bass_kernel_spmd(_CACHE['nc'], maps, core_ids=list(range(8)))
    out = np.zeros((4, S, D), dtype=np.float32)
    for core in range(8):
        b, hf = core // 2, core % 2
        out[b, hf * 1024:(hf + 1) * 1024] = res.results[core]["y"]
    return out
```

```python
import numpy as np
import concourse.bass as bass
import concourse.mybir as mybir
from concourse.bass_utils import run_bass_kernel_spmd

F32 = mybir.dt.float32
BF16 = mybir.dt.bfloat16
AF = mybir.ActivationFunctionType
ALU = mybir.AluOpType
AX = mybir.AxisListType

EPOCH = 30000


class Res:
    def __init__(self, name, t=None):
        self.name = name
        self.t = t
        self.w = None
        self.rs = []
        self.dsem = None
        self.dval = 0
        self.psum = False


class Prog:
    ENG = ('pe', 'act', 'dve', 'pool', 'sp')

    def __init__(self, nc):
        self.nc = nc
        self.q = {e: [] for e in self.ENG}
        self.cnt = {e: 0 for e in self.ENG}
        self.wm = {e: {} for e in self.ENG}
        self.sems = {}
        self.out_tokens = []
        self._stack = []
        self.nres = 0

    def sb(self, name, shape, dt):
        g = self.nc.sbuf_tensor(name, shape, dt)
        t = g.__enter__()
        self._stack.append(g)
        return Res(name, t)

    def ps(self, name, shape, dt):
        g = self.nc.psum_tensor(name, shape, dt)
        t = g.__enter__()
        self._stack.append(g)
        return Res(name, t)

    def arena(self, nbytes):
        g = self.nc.sbuf_tensor("arena", [128, nbytes], mybir.dt.uint8)
        self.ar = g.__enter__()
        self._stack.append(g)
        self.carved = []
        g2 = self.nc.psum_tensor("psar", [128, 4096], F32)
        self.par = g2.__enter__()
        self._stack.append(g2)
        self.pcarved = []

    def _alias(self, lst, r, off, nb):
        for (o, n, old) in lst:
            if o < off + nb and off < o + n:
                if old.w is not None:
                    r.rs.append(old.w)
                r.rs.extend(old.rs)
        lst.append((off, nb, r))

    def carve(self, name, off, shape, dt, parts=128):
        esz = 4 if dt == F32 else 2
        n = 1
        for d in shape[1:]:
            n *= d
        nb = n * esz
        assert off % 4 == 0 and off + nb <= self.ar.shape[1], (name, off, nb)
        t = self.ar[0:shape[0], off:off + nb].bitcast(dt)
        if len(shape) == 3:
            t = t.rearrange("p (a b) -> p a b", a=shape[1])
        elif len(shape) == 4:
            t = t.rearrange("p (a b c) -> p a b c", a=shape[1], b=shape[2])
        r = Res(name, t)
        self._alias(self.carved, r, off, nb)
        return r

    def pcarve(self, name, off, shape, dt):
        esz = 4 if dt == F32 else 2
        n = 1
        for d in shape[1:]:
            n *= d
        nb = n * esz
        assert off % 4 == 0 and off + nb <= 16384, (name, off, nb)
        t = self.par[0:shape[0], off // 4:(off + nb) // 4]
        if dt != F32:
            t = t.bitcast(dt)
        if len(shape) == 3:
            t = t.rearrange("p (a b) -> p a b", a=shape[1])
        elif len(shape) == 4:
            t = t.rearrange("p (a b c) -> p a b c", a=shape[1], b=shape[2])
        r = Res(name, t)
        r.psum = True
        self._alias(self.pcarved, r, off, nb)
        return r

    def sem(self, key):
        if key not in self.sems:
            g = self.nc.semaphore("s_%s_%s" % (str(key[0]), str(key[1])))
            h = g.__enter__()
            self._stack.append(g)
            self.sems[key] = h
        return self.sems[key]

    def _tok_semval(self, tok):
        if tok[0] == 'c':
            _, e, n = tok
            ep = (n - 1) // EPOCH
            return self.sem((e, ep)), (n - 1) % EPOCH + 1
        _, key, v = tok
        return self.sem(key), v

    def _deps(self, eng, reads, writes):
        need = []
        for r in reads:
            if r.w is not None:
                need.append(r.w)
            if r.psum:
                need.extend(t for t in r.rs if t[0] == 'c' and t[1] != eng)
        for w in writes:
            if w.w is not None:
                need.append(w.w)
            need.extend(w.rs)
        best = {}
        for tok in need:
            if tok[0] == 'c':
                if tok[1] == eng and eng == 'pe':
                    continue
                k = ('c', tok[1])
                v = tok[2]
            else:
                k = ('d', tok[1])
                v = tok[2]
            if self.wm[eng].get(k, 0) >= v:
                continue
            if best.get(k, 0) < v:
                best[k] = v
        waits = []
        for k, v in best.items():
            self.wm[eng][k] = v
            if k[0] == 'c':
                waits.append(('c', k[1], v))
            else:
                waits.append(('d', k[1], v))
        return waits

    def _mark(self, tok, reads, writes):
        for r in reads:
            r.rs.append(tok)
        for w in writes:
            w.w = tok
            w.rs = []

    def op(self, eng, fn, reads=(), writes=()):
        waits = [self._tok_semval(t) for t in self._deps(eng, reads, writes)]
        self.cnt[eng] += 1
        tok = ('c', eng, self.cnt[eng])
        sem, val = self._tok_semval(tok)
        self.wm[eng][('c', eng)] = max(self.wm[eng].get(('c', eng), 0), 0)

        def run(e, fn=fn, waits=waits, sem=sem):
            for s, v in waits:
                e.wait_ge(s, v)
            ins = fn(e)
            ins.then_inc(sem, 1)
        self.q[eng].append(run)
        self._mark(tok, reads, writes)
        return tok

    def dma(self, eng, out, in_, reads=(), writes=(), dres=None, is_out=False, n=1, fn=None):
        waits = [self._tok_semval(t) for t in self._deps(eng, reads, writes)]
        if dres.dsem is None:
            self.nres += 1
            dres.dsem = ('dma', self.nres)
        dres.dval += 16 * n
        tok = ('d', dres.dsem, dres.dval)
        sem, _ = self._tok_semval(tok)

        def run(e, waits=waits, sem=sem, fn=fn, out=out, in_=in_):
            for s, v in waits:
                e.wait_ge(s, v)
            if fn is None:
                e.dma_start(out=out, in_=in_).then_inc(sem, 16)
            else:
                for ins in fn(e):
                    ins.then_inc(sem, 16)
        self.q[eng].append(run)
        self._mark(tok, reads, writes)
        if is_out:
            self.out_tokens.append(tok)
        return tok

    def finish(self):
        best = {}
        for tok in self.out_tokens:
            best[tok[1]] = max(best.get(tok[1], 0), tok[2])
        waits = [self._tok_semval(('d', k, v)) for k, v in best.items()]

        def run(e, waits=waits):
            for s, v in waits:
                e.wait_ge(s, v)
        self.q['sp'].append(run)
        nc = self.nc
        q = self.q
        with nc.Block() as block:
            @block.sync
            def _(e):
                for f in q['sp']:
                    f(e)

            @block.scalar
            def _(e):
                for f in q['act']:
                    f(e)

            @block.vector
            def _(e):
                for f in q['dve']:
                    f(e)

            @block.gpsimd
            def _(e):
                for f in q['pool']:
                    f(e)

            @block.tensor
            def _(e):
                for f in q['pe']:
                    f(e)
        for g in reversed(self._stack):
            g.__exit__(None, None, None)


D = 2048
S = 2048
NOWN = 1024
EPS = 1e-6
PROJ = 6152
OFF_DAQ, OFF_DAK, OFF_DAV = 0, 1024, 2048
OFF_MLQ, OFF_MLK, OFF_MLV, OFF_MLO, OFF_MLG = 3072, 3584, 4096, 5120, 6144
ARENA = 207872
R_H, R_M, R_W, R_C = 0, 65536, 98304, 147456
R_T = R_C + 20480
R_T2 = R_T + 8448
C_IDENT, C_TRIBF, C_TRIUF, C_ONESF, C_COS, C_SIN, C_VALID = 0, 256, 512, 1024, 1536, 3584, 5632
C_MODFM, C_GMIX, C_GFFN, C_CSBF, C_CONVW, C_CONVB, C_BGATE, C_LAM = 5696, 5952, 6016, 6080, 6144, 6272, 6304, 6336
C_DANORM, C_MLNORM, C_GTM, C_GTF, C_MISC = 6400, 6912, 7936, 16128, 20224


def build_program(stage=99, dbg=None, peer_rows=16384):
    nc = bass.Bass("TRN2", target_bir_lowering=False)

    def din(name, shape, dt=F32):
        return nc.dram_tensor(name, list(shape), dt, kind="ExternalInput").ap()

    xw = din("xw", [S, D])
    cfm = din("cfm", [128, 16])
    w_ada = din("w_ada", [D, 6 * D])
    b_ada = din("b_ada", [1, 6 * D])
    w_adaf = din("w_adaf", [D, 2 * D])
    b_adaf = din("b_adaf", [1, 2 * D])
    gfin = din("gfin", [1, D])
    gmix_fm = din("gmix_fm", [128, 16])
    gffn_fm = din("gffn_fm", [128, 16])
    w_in = din("w_in", [D, PROJ])
    convw = din("convw", [128, 8, 4])
    convb = din("convb", [128, 8])
    bgate = din("bgate", [128, 8])
    lam4 = din("lam4", [128, 4, 64])
    danorm = din("danorm", [128, 128])
    mlnorm = din("mlnorm", [128, 256])
    w_out = din("w_out", [D, D])
    w_pq = din("w_pq", [D, D])
    subkT = din("subkT", [16, 128, 128])
    peer_u = din("peer_u", [peer_rows, D])
    peer_v = din("peer_v", [peer_rows, D])
    cosT = din("cosT", [128, 16, 32])
    sinT = din("sinT", [128, 16, 32])
    valid_d = din("valid", [128, 16])
    validrow_d = din("validrow", [128, S], BF16)
    ident_d = din("ident", [128, 128], BF16)
    tribf_d = din("tribf", [128, 128], BF16)
    triuf_d = din("triuf", [128, 128])
    y = nc.dram_tensor("y", [NOWN, D], F32, kind="ExternalOutput").ap()
    dbg_t = None
    if dbg is not None:
        dbg_t = nc.dram_tensor("dbg", list(dbg), F32, kind="ExternalOutput").ap()

    P = Prog(nc)
    P.arena(ARENA)
    LAM_INIT = 0.8 - 0.6 * 1.0

    def C(name, off, shape, dt):
        return P.carve(name, R_C + off, shape, dt)

    ident = C("ident", C_IDENT, [128, 128], BF16)
    tribf = C("tribf", C_TRIBF, [128, 128], BF16)
    triuf = C("triuf", C_TRIUF, [128, 128], F32)
    onesf = C("onesf", C_ONESF, [128, 128], F32)
    cos = C("cos", C_COS, [128, 16, 32], F32)
    sin = C("sin", C_SIN, [128, 16, 32], F32)
    valid = C("valid", C_VALID, [128, 16], F32)
    modfm = C("modfm", C_MODFM, [128, 4, 16], F32)
    gmix = C("gmix", C_GMIX, [128, 16], F32)
    gffn = C("gffn", C_GFFN, [128, 16], F32)
    csbf = C("csbf", C_MISC + 64, [128, 16], BF16)
    cw = C("convw", C_CONVW, [128, 8, 4], F32)
    cb = C("convb", C_CONVB, [128, 8], F32)
    bg = C("bgate", C_BGATE, [128, 8], F32)
    lam = C("lam", C_LAM, [128, 4], F32)
    dan = C("danorm", C_DANORM, [128, 128], F32)
    mln = C("mlnorm", C_MLNORM, [128, 256], F32)
    gtm = C("gtm", C_GTM, [128, 2048], F32)
    gtf = C("gtf", C_GTF, [128, 2048], BF16)
    misc = C("misc", C_MISC, [128, 64], F32)

    for (r, d) in ((ident, ident_d), (tribf, tribf_d), (triuf, triuf_d), (cos, cosT), (sin, sinT),
                   (valid, valid_d), (gmix, gmix_fm), (gffn, gffn_fm), (cw, convw), (cb, convb),
                   (bg, bgate), (dan, danorm), (mln, mlnorm)):
        P.dma('sp', r.t, d, writes=[r], dres=r)
    P.op('dve', lambda e: e.memset(onesf.t, 1.0), writes=[onesf])
    P.op('dve', lambda e: e.tensor_scalar(out=dan.t, in0=dan.t, scalar1=1.0 - LAM_INIT, scalar2=None, op0=ALU.mult),
         reads=[dan], writes=[dan])

    wring = [P.carve("w%d" % i, R_W + 16384 * i, [128, 16, 512], BF16) for i in range(3)]
    wstate = {'i': 0}

    def load_w(src, c0, ncol):
        r = wring[wstate['i'] % 3]
        wstate['i'] += 1
        P.dma('pool', r.t[:, :, 0:ncol], src[:, c0:c0 + ncol].rearrange("(k p) c -> p k c", p=128),
              writes=[r], dres=r)
        return r

    def PB(name, bank, shape, dt=F32, boff=0):
        return P.pcarve(name, bank * 2048 + boff, shape, dt)

    t_c = P.carve("t_c", R_T, [128, 16], F32)
    P.dma('sp', t_c.t, cfm, writes=[t_c], dres=t_c)
    P.op('act', lambda e: e.activation(out=csbf.t, in_=t_c.t, func=AF.Silu), reads=[t_c], writes=[csbf])
    brow = P.carve("brow", R_T + 35072, [1, 512], F32)
    mrow = P.carve("mrow", R_T + 37120, [1, 512], F32)
    one1 = P.carve("one1", R_T + 39168, [1, 128], F32)
    P.op('dve', lambda e: e.memset(one1.t, 1.0), writes=[one1])
    ps_row = PB("ps_row", 0, [1, 512])
    ps_fm = PB("ps_fm", 1, [128, 4])
    ps_bc = PB("ps_bc", 2, [128, 512])

    def ada_group(wsrc, bsrc, g):
        w = load_w(wsrc, g * 512, 512)
        P.dma('sp', brow.t, bsrc[:, g * 512:(g + 1) * 512], writes=[brow], dres=brow)

        def mm(e, w=w):
            ins = None
            for k in range(16):
                ins = e.matmul(ps_row.t, lhsT=csbf.t[:, k:k + 1], rhs=w.t[:, k, :], start=(k == 0), stop=(k == 15))
            return ins
        P.op('pe', mm, reads=[csbf, w], writes=[ps_row])
        P.op('dve', lambda e: e.tensor_tensor(out=mrow.t, in0=ps_row.t, in1=brow.t, op=ALU.add),
             reads=[ps_row, brow], writes=[mrow])

    def row_to_fm(dst_ap_fn, dst_res, add1=False):
        def mm(e):
            ins = None
            for c in range(4):
                ins = e.matmul(ps_fm.t[:, c:c + 1], lhsT=mrow.t[0:1, c * 128:(c + 1) * 128], rhs=one1.t[0:1, 0:1],
                               start=True, stop=True)
            return ins
        P.op('pe', mm, reads=[mrow, one1], writes=[ps_fm])
        if add1:
            P.op('dve', lambda e: e.tensor_scalar(out=dst_ap_fn(), in0=ps_fm.t, scalar1=1.0, scalar2=None, op0=ALU.add),
                 reads=[ps_fm], writes=[dst_res])
        else:
            P.op('dve', lambda e: e.tensor_copy(out=dst_ap_fn(), in_=ps_fm.t), reads=[ps_fm], writes=[dst_res])

    def row_to_bc(row_res, row_ap, dst_ap, dst_res, extra_reads=()):
        P.op('pe', lambda e: e.matmul(ps_bc.t, lhsT=one1.t[0:1, 0:128], rhs=row_ap, start=True, stop=True),
             reads=[row_res, one1], writes=[ps_bc])
        P.op('act', lambda e: e.copy(out=dst_ap, in_=ps_bc.t), reads=[ps_bc], writes=[dst_res])

    def ada_do(g):
        ada_group(w_ada, b_ada, g)
        vec, sub = g // 4, g % 4
        if vec in (0, 1, 3, 4):
            slot = {0: 0, 1: 1, 3: 2, 4: 3}[vec]
            row_to_fm(lambda slot=slot, sub=sub: modfm.t[:, slot, sub * 4:(sub + 1) * 4], modfm, add1=(vec in (1, 4)))
        elif vec == 2:
            row_to_bc(mrow, mrow.t[0:1, :], gtm.t[:, sub * 512:(sub + 1) * 512], gtm)
        else:
            row_to_bc(mrow, mrow.t[0:1, :], gtf.t[:, sub * 512:(sub + 1) * 512], gtf)
    for g in range(8):
        ada_do(g)
    P.op('dve', lambda e: e.tensor_tensor(out=modfm.t[:, 1, :], in0=modfm.t[:, 1, :], in1=gmix.t, op=ALU.mult),
         reads=[modfm, gmix], writes=[modfm])

    t_l4 = P.carve("t_l4", R_T + 20480, [128, 4, 64], F32)
    t_lp = P.carve("t_lp", R_T + 20480 + 1024, [128, 2, 64], F32)
    P.dma('sp', t_l4.t, lam4, writes=[t_l4], dres=t_l4)
    P.op('dve', lambda e: e.tensor_tensor(out=t_lp.t[:, 0, :], in0=t_l4.t[:, 0, :], in1=t_l4.t[:, 1, :], op=ALU.mult),
         reads=[t_l4], writes=[t_lp])
    P.op('dve', lambda e: e.tensor_tensor(out=t_lp.t[:, 1, :], in0=t_l4.t[:, 2, :], in1=t_l4.t[:, 3, :], op=ALU.mult),
         reads=[t_l4, t_lp], writes=[t_lp])
    P.op('dve', lambda e: e.tensor_reduce(out=lam.t[:, 0:2], in_=t_lp.t, axis=AX.X, op=ALU.add),
         reads=[t_lp], writes=[lam])
    P.op('act', lambda e: e.activation(out=lam.t[:, 2:4], in_=lam.t[:, 0:2], func=AF.Exp), reads=[lam], writes=[lam])
    P.op('dve', lambda e: e.tensor_tensor(out=lam.t[:, 0:1], in0=lam.t[:, 2:3], in1=lam.t[:, 3:4], op=ALU.subtract),
         reads=[lam], writes=[lam])
    P.op('dve', lambda e: e.tensor_scalar(out=lam.t[:, 0:1], in0=lam.t[:, 0:1], scalar1=LAM_INIT, scalar2=None, op0=ALU.add),
         reads=[lam], writes=[lam])

    xn = P.carve("xn", R_T, [128, 2048], BF16)
    junk = P.carve("junk", R_T + 4096, [128, 2048], BF16)
    ss = P.carve("ss", R_T + 8192, [128, 4], F32)
    xt = [P.carve("xt%d" % i, R_T2 + 8192 * i, [128, 2048], F32) for i in range(2)]
    ps_tr = [PB("ps_tr%d" % i, 6 + i, [128, 8, 128], BF16) for i in range(2)]

    def norm_to_fm(src_ap, src_res, dstT, tcol, slot_sh, slot_sc, tag):
        P.op('act', lambda e: e.activation(out=junk.t, in_=src_ap, func=AF.Square, accum_out=ss.t[:, 0:1]),
             reads=[src_res], writes=[junk, ss])
        P.op('dve', lambda e: e.tensor_scalar(out=ss.t[:, 1:2], in0=ss.t[:, 0:1], scalar1=1.0 / D, scalar2=EPS,
                                              op0=ALU.mult, op1=ALU.add), reads=[ss], writes=[ss])
        P.op('act', lambda e: e.activation(out=ss.t[:, 3:4], in_=ss.t[:, 1:2], func=AF.Sqrt), reads=[ss], writes=[ss])
        P.op('dve', lambda e: e.reciprocal(out=ss.t[:, 2:3], in_=ss.t[:, 3:4]), reads=[ss], writes=[ss])
        P.op('dve', lambda e: e.tensor_scalar(out=xn.t, in0=src_ap, scalar1=ss.t[:, 2:3], scalar2=None, op0=ALU.mult),
             reads=[src_res, ss], writes=[xn])
        for half in range(2):
            pt = ps_tr[half]

            def tr(e, half=half, pt=pt):
                ins = None
                for kk in range(8):
                    k = half * 8 + kk
                    ins = e.transpose(pt.t[:, kk, :], xn.t[:, k * 128:(k + 1) * 128], ident.t)
                return ins
            P.op('pe', tr, reads=[xn, ident], writes=[pt])
            for kk in range(8):
                k = half * 8 + kk
                if kk % 2 == 0:
                    P.op('dve', lambda e, k=k, kk=kk, pt=pt: e.tensor_scalar(
                        out=dstT.t[:, k, tcol:tcol + 128], in0=pt.t[:, kk, :],
                        scalar1=modfm.t[:, slot_sc, k:k + 1], scalar2=modfm.t[:, slot_sh, k:k + 1],
                        op0=ALU.mult, op1=ALU.add), reads=[pt, modfm], writes=[dstT])
                else:
                    P.op('act', lambda e, k=k, kk=kk, pt=pt: e.activation(
                        out=dstT.t[:, k, tcol:tcol + 128], in_=pt.t[:, kk, :], func=AF.Identity,
                        scale=modfm.t[:, slot_sc, k:k + 1], bias=modfm.t[:, slot_sh, k:k + 1]),
                        reads=[pt, modfm], writes=[dstT])

    hT = P.carve("hT", R_H, [128, 16, 2048], BF16)
    for tt in range(16):
        xb = xt[tt % 2]
        P.dma('sp', xb.t, xw[tt * 128:(tt + 1) * 128, :], writes=[xb], dres=xb)
        norm_to_fm(xb.t, xb, hT, tt * 128, 0, 1, "a%d" % tt)
    for g in range(8, 24):
        ada_do(g)
    P.op('dve', lambda e: e.tensor_tensor(out=modfm.t[:, 3, :], in0=modfm.t[:, 3, :], in1=gffn.t, op=ALU.mult),
         reads=[modfm, gffn], writes=[modfm])

    ctx = dict(nc=nc, P=P, PB=PB, load_w=load_w, hT=hT, y=y, dbg_t=dbg_t, xw=xw, w_in=w_in, ident=ident,
               tribf=tribf, triuf=triuf, onesf=onesf, cos=cos, sin=sin, valid=valid, modfm=modfm, cw=cw, cb=cb,
               bg=bg, lam=lam, dan=dan, mln=mln, gtm=gtm, gtf=gtf, misc=misc, w_out=w_out, w_pq=w_pq,
               subkT=subkT, peer_u=peer_u, peer_v=peer_v, validrow_d=validrow_d, w_adaf=w_adaf, b_adaf=b_adaf,
               gfin=gfin, wring=wring, wstate=wstate, norm_to_fm=norm_to_fm, one1=one1, xt=xt, stage=stage, csbf=csbf, ps_tr=ps_tr)
    return ctx


def host_inputs(x, c, w_ada, b_ada, g_mix, w_in, conv_w, conv_b, b_igate, b_fgate,
                lambda_q1, lambda_k1, lambda_q2, lambda_k2, da_norm, ml_norm, w_out,
                g_ffn, w_pq, sub_keys, peer_u, peer_v, w_ada_final, b_ada_final, g_final):
    import ml_dtypes
    f32 = np.float32
    A = lambda a: np.ascontiguousarray(np.asarray(a, dtype=f32))
    x = A(x); c = A(c)

    def fm(v):
        return np.ascontiguousarray(A(v).reshape(16, 128).T)

    def rep(v, n=128):
        v = A(v).reshape(1, -1)
        return np.ascontiguousarray(np.broadcast_to(v, (n, v.shape[1])))

    half = 32
    inv = (10000.0 ** (-np.arange(half, dtype=np.float32) / half)).astype(np.float32)
    ident = np.eye(128, dtype=np.float32)
    triu = np.triu(np.ones((128, 128), dtype=np.float32))
    shared = {
        "w_ada": A(w_ada)[0], "b_ada": A(b_ada)[0].reshape(1, -1),
        "w_adaf": A(w_ada_final), "b_adaf": A(b_ada_final).reshape(1, -1), "gfin": A(g_final).reshape(1, -1),
        "gmix_fm": fm(np.asarray(g_mix)[0]), "gffn_fm": fm(np.asarray(g_ffn)[0]),
        "w_in": A(w_in)[0],
        "convw": np.ascontiguousarray(A(conv_w)[0].T.reshape(8, 128, 4).transpose(1, 0, 2)),
        "convb": np.ascontiguousarray(A(conv_b)[0].reshape(8, 128).T),
        "bgate": rep(np.concatenate([np.asarray(b_igate)[0], np.asarray(b_fgate)[0]])),
        "lam4": np.ascontiguousarray(np.stack([rep(np.asarray(v)[0]) for v in (lambda_q1, lambda_k1, lambda_q2, lambda_k2)], axis=1)),
        "danorm": rep(np.asarray(da_norm)[0]), "mlnorm": rep(np.asarray(ml_norm)[0]),
        "w_out": A(w_out)[0], "w_pq": A(w_pq)[0],
        "subkT": np.ascontiguousarray(A(sub_keys)[0].reshape(16, 128, 128).transpose(0, 2, 1)),
        "peer_u": A(peer_u)[0], "peer_v": A(peer_v)[0],
        "ident": ident.astype(ml_dtypes.bfloat16), "tribf": triu.astype(ml_dtypes.bfloat16), "triuf": triu,
    }
    maps = []
    for core in range(8):
        b, hf = core // 2, core % 2
        m = dict(shared)
        if hf == 1:
            xwin = x[b]
            pos = np.arange(S, dtype=np.float32)
            val = np.ones(S, dtype=np.float32)
        else:
            xwin = np.concatenate([np.zeros((1024, D), f32), x[b, :1024]], axis=0)
            pos = np.concatenate([np.zeros(1024, f32), np.arange(1024, dtype=f32)])
            val = np.concatenate([np.zeros(1024, f32), np.ones(1024, f32)])
        ang = pos[:, None] * inv[None, :]
        m["xw"] = np.ascontiguousarray(xwin)
        m["cfm"] = fm(c[b])
        m["cosT"] = np.ascontiguousarray(np.cos(ang).astype(f32).reshape(16, 128, 32).transpose(1, 0, 2))
        m["sinT"] = np.ascontiguousarray(np.sin(ang).astype(f32).reshape(16, 128, 32).transpose(1, 0, 2))
        m["valid"] = np.ascontiguousarray(val.reshape(16, 128).T)
        m["validrow"] = np.ascontiguousarray(np.broadcast_to(val[None, :], (128, S))).astype(ml_dtypes.bfloat16)
        maps.append(m)
    return maps


def phase_mixers(ctx):
    P, PB, load_w, hT, w_in = ctx['P'], ctx['PB'], ctx['load_w'], ctx['hT'], ctx['w_in']
    ident, tribf, cos, sin, valid, lam, dan, mln = (ctx[k] for k in ('ident', 'tribf', 'cos', 'sin', 'valid', 'lam', 'dan', 'mln'))
    ps_tr = ctx['ps_tr']
    mixT = P.carve("mixT", R_M, [128, 16, 1024], BF16)
    ctx['mixT'] = mixT
    sso = P.carve("sso", R_T2, [128, 4], F32)
    jk = P.carve("jk", R_T2 + 16, [128, 256], F32)
    B = R_T2 + 1056
    kT = P.carve("kT", B, [128, 2048], BF16)
    qT = P.carve("qT", B + 4096, [128, 1024], BF16)
    V = P.carve("V", B + 6144, [128, 16, 144], BF16)
    rot = [P.carve("rot%d" % i, B + 10752 + 256 * i, [128, 2, 2, 32], BF16) for i in range(2)]
    tA = P.carve("tA", B + 11264, [128, 2, 32], F32)
    tB = P.carve("tB", B + 11520, [128, 2, 32], F32)
    pt = [P.carve("pt%d" % i, B + 11776 + 1024 * i, [128, 4, 128], BF16) for i in range(2)]
    rz = P.carve("rz", B + 13824, [128, 4], F32)
    oa = P.carve("oa", B + 13888, [128, 128], F32)
    t2 = P.carve("t2", B + 14400, [128, 128], F32)
    oan = [P.carve("oan%d" % i, B + 14912 + 256 * i, [128, 128], BF16) for i in range(2)]
    ps_p = [PB("ps_p%d" % i, i, [128, 512]) for i in range(2)]
    ps_s = [PB("ps_s%d" % i, 2 + i, [128, 4, 128]) for i in range(2)]
    ps_o = [PB("ps_o%d" % i, 4 + i, [128, 2, 129]) for i in range(2)]
    cnt = {'p': 0, 's': 0, 'o': 0, 'r': 0, 't': 0, 'pt': 0, 'n': 0}

    def nxt(lst, key):
        r = lst[cnt[key] % len(lst)]
        cnt[key] += 1
        return r

    def proj_tok(w, c0, ncol, tt, psr, pcol):
        def mm(e):
            ins = None
            for kk in range(16):
                ins = e.matmul(psr.t[:, pcol:pcol + ncol], lhsT=hT.t[:, kk, tt * 128:(tt + 1) * 128],
                               rhs=w.t[:, kk, c0:c0 + ncol], start=(kk == 0), stop=(kk == 15))
            return ins
        P.op('pe', mm, reads=[hT, w], writes=[psr])

    def rope_T(psr, pcol, tt, dstT, dcol):
        src = psr.t[:, pcol:pcol + 128].rearrange("p (m h d) -> p m h d", m=2, h=2)
        cb_ = cos.t[:, tt:tt + 1, :].to_broadcast([128, 2, 32])
        sb_ = sin.t[:, tt:tt + 1, :].to_broadcast([128, 2, 32])
        r = nxt(rot, 'r')
        P.op('dve', lambda e: e.tensor_tensor(out=tA.t, in0=src[:, :, 0, :], in1=cb_, op=ALU.mult), reads=[psr, cos], writes=[tA])
        P.op('dve', lambda e: e.tensor_tensor(out=tB.t, in0=src[:, :, 1, :], in1=sb_, op=ALU.mult), reads=[psr, sin], writes=[tB])
        P.op('dve', lambda e: e.tensor_tensor(out=r.t[:, :, 0, :], in0=tA.t, in1=tB.t, op=ALU.subtract), reads=[tA, tB], writes=[r])
        P.op('dve', lambda e: e.tensor_tensor(out=tA.t, in0=src[:, :, 1, :], in1=cb_, op=ALU.mult), reads=[psr, cos, r], writes=[tA])
        P.op('dve', lambda e: e.tensor_tensor(out=tB.t, in0=src[:, :, 0, :], in1=sb_, op=ALU.mult), reads=[psr, sin, r], writes=[tB])
        P.op('dve', lambda e: e.tensor_tensor(out=r.t[:, :, 1, :], in0=tA.t, in1=tB.t, op=ALU.add), reads=[tA, tB], writes=[r])
        tr = nxt(ps_tr, 't')
        P.op('pe', lambda e: e.transpose(tr.t[:, 0, :], r.t.rearrange("p m h d -> p (m h d)"), ident.t), reads=[r, ident], writes=[tr])
        P.op('act', lambda e: e.copy(out=dstT.t[:, dcol:dcol + 128], in_=tr.t[:, 0, :]), reads=[tr], writes=[dstT])

    def rms_scale(src_ap, src_res, n, out_col):
        P.op('act', lambda e: e.activation(out=jk.t[:, 0:n], in_=src_ap, func=AF.Square, accum_out=sso.t[:, 0:1]),
             reads=[src_res], writes=[jk, sso])
        P.op('dve', lambda e: e.tensor_scalar(out=sso.t[:, 1:2], in0=sso.t[:, 0:1], scalar1=1.0 / n, scalar2=EPS,
                                              op0=ALU.mult, op1=ALU.add), reads=[sso], writes=[sso])
        P.op('act', lambda e: e.activation(out=sso.t[:, 2:3], in_=sso.t[:, 1:2], func=AF.Ln), reads=[sso], writes=[sso])
        P.op('act', lambda e: e.activation(out=sso.t[:, out_col:out_col + 1], in_=sso.t[:, 2:3], func=AF.Exp, scale=-0.5), reads=[sso], writes=[sso])

    P.op('dve', lambda e: e.tensor_copy(out=V.t[:, :, 128:129], in_=valid.t.unsqueeze(2)), reads=[valid], writes=[V])
    for grp in range(2):
        wq = load_w(w_in, OFF_DAQ + grp * 512, 512)
        wk = load_w(w_in, OFF_DAK + grp * 512, 512)
        wv = load_w(w_in, OFF_DAV + grp * 512, 512)
        for hl in range(4):
            hd = grp * 4 + hl
            c0 = hl * 128
            def post(pp, tt):
                rope_T(pp, 0, tt, kT, tt * 128)
                P.op('act', lambda e: e.activation(out=V.t[:, tt, 0:128], in_=pp.t[:, 128:256], func=AF.Identity,
                                                   scale=valid.t[:, tt:tt + 1]), reads=[pp, valid], writes=[V])
                if tt >= 8:
                    rope_T(pp, 256, tt, qT, (tt - 8) * 128)
            pend = None
            for tt in range(16):
                pp = nxt(ps_p, 'p')
                proj_tok(wk, c0, 128, tt, pp, 0)
                proj_tok(wv, c0, 128, tt, pp, 128)
                if tt >= 8:
                    proj_tok(wq, c0, 128, tt, pp, 256)
                if pend is not None:
                    post(*pend)
                pend = (pp, tt)
            post(*pend)
            if ctx.get('stop') == 'proj':
                ctx['kT'], ctx['qT'], ctx['V'] = kT, qT, V
                return
            groups = []
            for qi in range(8):
                nk = 9 + qi
                for m in range(2):
                    for g0 in range(0, nk, 4):
                        groups.append((qi, m, list(range(g0, min(g0 + 4, nk))), nk))
            gbuf = {}

            def emit_S(i):
                qi, m, ks, nk = groups[i]
                psb = nxt(ps_s, 's')
                ptb = nxt(pt, 'pt')
                gbuf[i] = (psb, ptb)

                def mm(e):
                    ins = None
                    for j, kj in enumerate(ks):
                        ins = e.matmul(psb.t[:, j, :], lhsT=kT.t[m * 64:(m + 1) * 64, kj * 128:(kj + 1) * 128],
                                       rhs=qT.t[m * 64:(m + 1) * 64, qi * 128:(qi + 1) * 128], start=True, stop=True)
                    return ins
                P.op('pe', mm, reads=[kT, qT], writes=[psb])
            emit_S(0)
            po = None
            for i in range(len(groups)):
                qi, m, ks, nk = groups[i]
                if m == 0 and ks[0] == 0:
                    po = nxt(ps_o, 'o')
                if i + 1 < len(groups):
                    emit_S(i + 1)
                psb, ptb = gbuf.pop(i)
                n = len(ks)
                P.op('act', lambda e, psb=psb, ptb=ptb, n=n: e.activation(out=ptb.t[:, 0:n, :], in_=psb.t[:, 0:n, :],
                                                                          func=AF.Exp, scale=0.125), reads=[psb], writes=[ptb])
                if (8 + qi) in ks:
                    j = ks.index(8 + qi)
                    P.op('dve', lambda e, ptb=ptb, j=j: e.tensor_tensor(out=ptb.t[:, j, :], in0=ptb.t[:, j, :], in1=tribf.t,
                                                                        op=ALU.mult), reads=[ptb, tribf], writes=[ptb])

                def av(e, ks=ks, ptb=ptb, m=m, po=po, nk=nk):
                    ins = None
                    for j, kj in enumerate(ks):
                        ins = e.matmul(po.t[:, m, :], lhsT=ptb.t[:, j, :], rhs=V.t[:, kj, 0:129],
                                       start=(kj == 0), stop=(kj == nk - 1))
                    return ins
                P.op('pe', av, reads=[ptb, V], writes=[po])
                if not (m == 1 and ks[-1] == nk - 1):
                    continue
                P.op('dve', lambda e, po=po: e.reciprocal(out=rz.t[:, 0:2], in_=po.t[:, :, 128]), reads=[po], writes=[rz])
                P.op('dve', lambda e: e.tensor_tensor(out=rz.t[:, 1:2], in0=rz.t[:, 1:2], in1=lam.t[:, 0:1], op=ALU.mult),
                     reads=[rz, lam], writes=[rz])
                P.op('act', lambda e, po=po: e.activation(out=t2.t, in_=po.t[:, 1, 0:128], func=AF.Identity, scale=rz.t[:, 1:2]),
                     reads=[po, rz], writes=[t2])
                P.op('dve', lambda e, po=po: e.scalar_tensor_tensor(out=oa.t, in0=po.t[:, 0, 0:128], scalar=rz.t[:, 0:1], in1=t2.t,
                                                                    op0=ALU.mult, op1=ALU.subtract), reads=[po, rz, t2], writes=[oa])
                rms_scale(oa.t, oa, 128, 3)
                on = nxt(oan, 'n')
                P.op('dve', lambda e, on=on: e.scalar_tensor_tensor(out=on.t, in0=oa.t, scalar=sso.t[:, 3:4], in1=dan.t,
                                                                    op0=ALU.mult, op1=ALU.mult), reads=[oa, sso, dan], writes=[on])
                tr = nxt(ps_tr, 't')
                P.op('pe', lambda e, tr=tr, on=on: e.transpose(tr.t[:, 0, :], on.t, ident.t), reads=[on, ident], writes=[tr])
                P.op('act', lambda e, tr=tr, hd=hd, qi=qi: e.copy(out=mixT.t[:, hd, qi * 128:(qi + 1) * 128], in_=tr.t[:, 0, :]),
                     reads=[tr], writes=[mixT])
    ctx['sso'] = sso
    ctx['jk'] = jk
    ctx['rms_scale'] = rms_scale
    ctx['nxt'] = nxt
    ctx['cnt'] = cnt
    ctx['ps_p'] = ps_p
    ctx['proj_tok'] = proj_tok


def phase_mlstm(ctx):
    import math
    P, PB, hT, w_in = ctx['P'], ctx['PB'], ctx['hT'], ctx['w_in']
    ident, tribf, triuf, onesf, valid, mln, cw, cb, bg = (ctx[k] for k in ('ident', 'tribf', 'triuf', 'onesf', 'valid', 'mln', 'cw', 'cb', 'bg'))
    ps_tr, mixT, nxt, sso, rms_scale = ctx['ps_tr'], ctx['mixT'], ctx['nxt'], ctx['sso'], ctx['rms_scale']
    wring, wstate = ctx['wring'], ctx['wstate']
    B = R_T2 + 1056
    G0 = B
    wg = P.carve("wg", G0, [128, 16, 8], BF16)
    graw = P.carve("graw", G0 + 256, [128, 16, 8], F32)
    gi = P.carve("gi", G0 + 768, [128, 16, 4], F32)
    gf = P.carve("gf", G0 + 1024, [128, 16, 4], F32)
    gA = P.carve("gA", G0 + 1280, [128, 16, 4], F32)
    gea = P.carve("gea", G0 + 1536, [128, 16, 4], F32)
    geft = P.carve("geft", G0 + 1792, [128, 16, 4], F32)
    gtmp = P.carve("gtmp", G0 + 2048, [128, 16, 4], F32)
    vm1 = P.carve("vm1", G0 + 2304, [128, 16], F32)
    B2 = G0 + 2368 + 8
    cur = {'o': B2}

    def AL(name, shape, dt):
        esz = 4 if dt == F32 else 2
        n = 1
        for d in shape[1:]:
            n *= d
        off = cur['o']
        cur['o'] = off + ((n * esz + 63) // 64) * 64
        return P.carve(name, off, shape, dt)
    pre = AL("pre", [128, 2052], BF16)
    acc = AL("acc", [128, 512], F32)
    qmT = AL("qmT", [128, 1024], BF16)
    kmT = AL("kmT", [128, 2048], BF16)
    Kt = AL("Kt", [128, 16, 128], BF16)
    Vp = [AL("Vp%d" % i, [128, 260], BF16) for i in range(2)]
    Cst = AL("Cst", [128, 257], F32)
    Cbf = AL("Cbf", [128, 258], BF16)
    Pt = [AL("Pt%d" % i, [128, 128], BF16) for i in range(2)]
    og = AL("og", [128, 256], F32)
    hh = AL("hh", [128, 256], F32)
    om = AL("om", [128, 256], BF16)
    dsc = AL("dsc", [128, 4], F32)
    vrow = AL("vrow", [128, 2048], BF16)
    assert cur['o'] <= ARENA, cur['o']
    P.dma('sp', vrow.t, ctx['validrow_d'], writes=[vrow], dres=vrow)

    ps_a = [PB("pm_a%d" % i, i, [128, 512]) for i in range(2)]
    ps_s = [PB("pm_s%d" % i, 2 + i, [128, 512]) for i in range(2)]
    ps_n = [PB("pm_n%d" % i, 4 + i, [128, 512]) for i in range(2)]
    c2 = {'a': 0, 's': 0, 'n': 0, 'vp': 0, 'pt': 0}

    def nx(lst, k):
        r = lst[c2[k] % len(lst)]
        c2[k] += 1
        return r

    P.dma('pool', wg.t, w_in[:, OFF_MLG:OFF_MLG + 8].rearrange("(k p) c -> p k c", p=128), writes=[wg], dres=wg)
    pg = nx(ps_a, 'a')

    def gmm(e):
        ins = None
        for tt in range(16):
            for kk in range(16):
                ins = e.matmul(pg.t[:, tt * 8:(tt + 1) * 8], lhsT=hT.t[:, kk, tt * 128:(tt + 1) * 128], rhs=wg.t[:, kk, :],
                               start=(kk == 0), stop=(kk == 15))
        return ins
    P.op('pe', gmm, reads=[hT, wg], writes=[pg])
    P.op('dve', lambda e: e.tensor_tensor(out=graw.t, in0=pg.t[:, 0:128].rearrange("p (t g) -> p t g", g=8),
                                          in1=bg.t.unsqueeze(1).to_broadcast([128, 16, 8]), op=ALU.add),
         reads=[pg, bg], writes=[graw])
    P.op('act', lambda e: e.activation(out=graw.t, in_=graw.t, func=AF.Tanh, scale=1.0 / 15.0), reads=[graw], writes=[graw])
    vb = valid.t.unsqueeze(2).to_broadcast([128, 16, 4])
    P.op('dve', lambda e: e.tensor_scalar(out=vm1.t, in0=valid.t, scalar1=-1.0, scalar2=1.0e4, op0=ALU.add, op1=ALU.mult),
         reads=[valid], writes=[vm1])
    P.op('dve', lambda e: e.scalar_tensor_tensor(out=gi.t, in0=graw.t[:, :, 0:4], scalar=15.0, in1=vb, op0=ALU.mult, op1=ALU.mult),
         reads=[graw, valid], writes=[gi])
    P.op('dve', lambda e: e.tensor_tensor(out=gi.t, in0=gi.t, in1=vm1.t.unsqueeze(2).to_broadcast([128, 16, 4]), op=ALU.add),
         reads=[gi, vm1], writes=[gi])
    P.op('act', lambda e: e.activation(out=gtmp.t, in_=graw.t[:, :, 4:8], func=AF.Exp, scale=-15.0), reads=[graw], writes=[gtmp])
    P.op('dve', lambda e: e.tensor_scalar(out=gtmp.t, in0=gtmp.t, scalar1=1.0, scalar2=None, op0=ALU.add), reads=[gtmp], writes=[gtmp])
    P.op('act', lambda e: e.activation(out=gtmp.t, in_=gtmp.t, func=AF.Ln), reads=[gtmp], writes=[gtmp])
    P.op('dve', lambda e: e.scalar_tensor_tensor(out=gf.t, in0=gtmp.t, scalar=-1.0, in1=vb, op0=ALU.mult, op1=ALU.mult),
         reads=[gtmp, valid], writes=[gf])
    pf = nx(ps_a, 'a')
    gf2 = gf.t.rearrange("p t g -> p (t g)")
    P.op('pe', lambda e: e.matmul(pf.t[:, 0:64], lhsT=triuf.t, rhs=gf2, start=True, stop=True), reads=[triuf, gf], writes=[pf])
    P.op('pe', lambda e: e.matmul(pf.t[:, 64:128], lhsT=onesf.t, rhs=gf2, start=True, stop=True), reads=[onesf, gf], writes=[pf])
    P.op('act', lambda e: e.activation(out=gA.t.rearrange("p t g -> p (t g)"), in_=pf.t[:, 0:64], func=AF.Exp), reads=[pf], writes=[gA])
    P.op('act', lambda e: e.activation(out=geft.t.rearrange("p t g -> p (t g)"), in_=pf.t[:, 64:128], func=AF.Exp), reads=[pf], writes=[geft])
    P.op('dve', lambda e: e.tensor_tensor(out=gtmp.t.rearrange("p t g -> p (t g)"), in0=gi.t.rearrange("p t g -> p (t g)"),
                                          in1=pf.t[:, 0:64], op=ALU.subtract), reads=[gi, pf, gtmp], writes=[gtmp])
    P.op('dve', lambda e: e.tensor_scalar(out=gtmp.t, in0=gtmp.t, scalar1=-0.5 * math.log(128.0), scalar2=None, op0=ALU.add),
         reads=[gtmp], writes=[gtmp])
    P.op('act', lambda e: e.activation(out=gea.t, in_=gtmp.t, func=AF.Exp), reads=[gtmp], writes=[gea])

    P.op('dve', lambda e: e.memset(pre.t[:, 0:4], 0.0), writes=[pre])

    def loadw2(pieces):
        r = wring[wstate['i'] % 3]
        wstate['i'] += 1

        def fn(e):
            return [e.dma_start(out=r.t[:, :, d0:d0 + n], in_=w_in[:, c0:c0 + n].rearrange("(k p) c -> p k c", p=128))
                    for (c0, n, d0) in pieces]
        P.dma('pool', None, None, writes=[r], dres=r, n=len(pieces), fn=fn)
        return r

    for hd in range(4):
        wA = loadw2([(OFF_MLQ + hd * 128, 128, 0), (OFF_MLK + hd * 128, 128, 128)])
        wB = loadw2([(OFF_MLV + hd * 256, 256, 0), (OFF_MLO + hd * 256, 256, 256)])
        for which in range(2):
            ch = which * 4 + hd
            for tc in range(4):
                pa = nx(ps_a, 'a')

                def mm(e, pa=pa, which=which, tc=tc, wA=wA):
                    ins = None
                    for kk in range(16):
                        ins = e.matmul(pa.t, lhsT=wA.t[:, kk, which * 128:(which + 1) * 128], rhs=hT.t[:, kk, tc * 512:(tc + 1) * 512],
                                       start=(kk == 0), stop=(kk == 15))
                    return ins
                P.op('pe', mm, reads=[wA, hT], writes=[pa])
                P.op('dve', lambda e, pa=pa, tc=tc: e.tensor_tensor(out=pre.t[:, 4 + tc * 512:4 + (tc + 1) * 512], in0=pa.t,
                                                                    in1=vrow.t[:, tc * 512:(tc + 1) * 512], op=ALU.mult),
                     reads=[pa, vrow], writes=[pre])
            for tc in range(4):
                if which == 0 and tc < 2:
                    continue
                b0 = 4 + tc * 512 - 3
                P.op('dve', lambda e, b0=b0, ch=ch: e.tensor_scalar(out=acc.t, in0=pre.t[:, b0:b0 + 512], scalar1=cw.t[:, ch, 0:1],
                                                                    scalar2=None, op0=ALU.mult), reads=[pre, cw], writes=[acc])
                for w_ in range(1, 4):
                    P.op('dve', lambda e, b0=b0, ch=ch, w_=w_: e.scalar_tensor_tensor(
                        out=acc.t, in0=pre.t[:, b0 + w_:b0 + w_ + 512], scalar=cw.t[:, ch, w_:w_ + 1], in1=acc.t,
                        op0=ALU.mult, op1=ALU.add), reads=[pre, cw, acc], writes=[acc])
                if which == 0:
                    dst, dres = qmT.t[:, (tc - 2) * 512:(tc - 1) * 512], qmT
                else:
                    dst, dres = kmT.t[:, tc * 512:(tc + 1) * 512], kmT
                P.op('act', lambda e, dst=dst, ch=ch: e.activation(out=dst, in_=acc.t, func=AF.Silu, bias=cb.t[:, ch:ch + 1]),
                     reads=[acc, cb], writes=[dres])
        for tt in range(16):
            tr = nxt(ps_tr, 't')
            P.op('pe', lambda e, tr=tr, tt=tt: e.transpose(tr.t[:, 0, :], kmT.t[:, tt * 128:(tt + 1) * 128], ident.t),
                 reads=[kmT, ident], writes=[tr])
            P.op('act', lambda e, tr=tr, tt=tt: e.copy(out=Kt.t[:, tt, :], in_=tr.t[:, 0, :]), reads=[tr], writes=[Kt])
        P.op('dve', lambda e: e.memset(Cst.t, 0.0), writes=[Cst])
        P.op('dve', lambda e: e.memset(Cbf.t, 0.0), writes=[Cbf])
        for tt in range(16):
            col = tt * 4 + hd
            own = tt >= 8
            qi = tt - 8
            vp = nx(Vp, 'vp')
            pa = nx(ps_a, 'a')

            def vmm(e, pa=pa, tt=tt, wB=wB, own=own):
                ins = None
                n = 512 if own else 256
                for kk in range(16):
                    ins = e.matmul(pa.t[:, 0:n], lhsT=hT.t[:, kk, tt * 128:(tt + 1) * 128], rhs=wB.t[:, kk, 0:n],
                                   start=(kk == 0), stop=(kk == 15))
                return ins
            P.op('pe', vmm, reads=[hT, wB], writes=[pa])
            ea_ap = gea.t[:, tt, hd:hd + 1]
            P.op('act', lambda e, pa=pa, vp=vp, ea_ap=ea_ap: e.activation(out=vp.t[:, 0:256], in_=pa.t[:, 0:256], func=AF.Identity, scale=ea_ap),
                 reads=[pa, gea], writes=[vp])
            P.op('dve', lambda e, vp=vp, ea_ap=ea_ap: e.tensor_copy(out=vp.t[:, 256:257], in_=ea_ap), reads=[gea], writes=[vp])
            if own:
                P.op('act', lambda e, pa=pa: e.activation(out=og.t, in_=pa.t[:, 256:512], func=AF.Sigmoid), reads=[pa], writes=[og])
                psb = nx(ps_s, 's')
                P.op('pe', lambda e, psb=psb, tt=tt, qi=qi: e.matmul(psb.t[:, 0:128], lhsT=kmT.t[:, tt * 128:(tt + 1) * 128],
                                                                     rhs=qmT.t[:, qi * 128:(qi + 1) * 128], start=True, stop=True),
                     reads=[kmT, qmT], writes=[psb])
                ptb = nx(Pt, 'pt')
                P.op('dve', lambda e, psb=psb, ptb=ptb: e.tensor_tensor(out=ptb.t, in0=psb.t[:, 0:128], in1=tribf.t, op=ALU.mult),
                     reads=[psb, tribf], writes=[ptb])
                pn = nx(ps_n, 'n')

                def nmm(e, pn=pn, qi=qi, ptb=ptb, vp=vp):
                    e.matmul(pn.t[:, 0:257], lhsT=qmT.t[:, qi * 128:(qi + 1) * 128], rhs=Cbf.t[:, 0:257], start=True, stop=False)
                    return e.matmul(pn.t[:, 0:257], lhsT=ptb.t, rhs=vp.t[:, 0:257], start=False, stop=True)
                P.op('pe', nmm, reads=[qmT, Cbf, ptb, vp], writes=[pn])
                A_ap = gA.t[:, tt, hd:hd + 1]
                P.op('dve', lambda e, pn=pn: e.tensor_scalar(out=dsc.t[:, 2:3], in0=pn.t[:, 256:257], scalar1=-1.0, scalar2=None,
                                                             op0=ALU.mult), reads=[pn], writes=[dsc])
                P.op('dve', lambda e, pn=pn: e.tensor_tensor(out=dsc.t[:, 2:3], in0=dsc.t[:, 2:3], in1=pn.t[:, 256:257], op=ALU.max),
                     reads=[pn, dsc], writes=[dsc])
                P.op('dve', lambda e, A_ap=A_ap: e.tensor_tensor(out=dsc.t[:, 0:1], in0=dsc.t[:, 2:3], in1=A_ap, op=ALU.mult),
                     reads=[dsc, gA], writes=[dsc])
                P.op('dve', lambda e: e.tensor_scalar(out=dsc.t[:, 1:2], in0=dsc.t[:, 0:1], scalar1=1.0, scalar2=None, op0=ALU.max),
                     reads=[dsc], writes=[dsc])
                P.op('dve', lambda e: e.reciprocal(out=dsc.t[:, 2:3], in_=dsc.t[:, 1:2]), reads=[dsc], writes=[dsc])
                P.op('dve', lambda e, A_ap=A_ap: e.tensor_tensor(out=dsc.t[:, 3:4], in0=dsc.t[:, 2:3], in1=A_ap, op=ALU.mult),
                     reads=[dsc, gA], writes=[dsc])
                P.op('act', lambda e, pn=pn: e.activation(out=hh.t, in_=pn.t[:, 0:256], func=AF.Identity, scale=dsc.t[:, 3:4]),
                     reads=[pn, dsc], writes=[hh])
                rms_scale(hh.t, hh, 256, 3)
                P.op('dve', lambda e: e.scalar_tensor_tensor(out=hh.t, in0=hh.t, scalar=sso.t[:, 3:4], in1=mln.t, op0=ALU.mult, op1=ALU.mult),
                     reads=[hh, sso, mln], writes=[hh])
                P.op('dve', lambda e: e.tensor_tensor(out=om.t, in0=hh.t, in1=og.t, op=ALU.mult), reads=[hh, og], writes=[om])
                for half in range(2):
                    tr = nxt(ps_tr, 't')
                    P.op('pe', lambda e, tr=tr, half=half: e.transpose(tr.t[:, 0, :], om.t[:, half * 128:(half + 1) * 128], ident.t),
                         reads=[om, ident], writes=[tr])
                    P.op('act', lambda e, tr=tr, half=half, hd=hd, qi=qi: e.copy(
                        out=mixT.t[:, 8 + hd * 2 + half, qi * 128:(qi + 1) * 128], in_=tr.t[:, 0, :]), reads=[tr], writes=[mixT])
            if tt < 15:
                pu = nx(ps_n, 'n')
                P.op('pe', lambda e, pu=pu, tt=tt, vp=vp: e.matmul(pu.t[:, 0:257], lhsT=Kt.t[:, tt, :], rhs=vp.t[:, 0:257], start=True, stop=True),
                     reads=[Kt, vp], writes=[pu])
                P.op('dve', lambda e, pu=pu: e.tensor_tensor(out=Cst.t, in0=Cst.t, in1=pu.t[:, 0:257], op=ALU.add), reads=[Cst, pu], writes=[Cst])
                P.op('dve', lambda e, tt=tt, hd=hd: e.tensor_scalar(out=Cst.t, in0=Cst.t, scalar1=geft.t[:, tt, hd:hd + 1], scalar2=None, op0=ALU.mult),
                     reads=[Cst, geft], writes=[Cst])
                P.op('act', lambda e: e.copy(out=Cbf.t[:, 0:257], in_=Cst.t), reads=[Cst], writes=[Cbf])


def phase_wout_norm2(ctx):
    P, PB, load_w = ctx['P'], ctx['PB'], ctx['load_w']
    mixT, gtm, xw = ctx['mixT'], ctx['gtm'], ctx['xw']
    x1 = P.carve("x1", R_H, [128, 8, 2048], F32)
    ctx['x1'] = x1
    xs = [P.carve("xs%d" % i, R_T2 + 2048 * i, [128, 512], F32) for i in range(4)]
    ps = [PB("pw%d" % i, i, [128, 512]) for i in range(4)]
    n = 0
    for cg in range(4):
        w = load_w(ctx['w_out'], cg * 512, 512)
        for qi in range(8):
            pp = ps[n % 4]
            xb = xs[n % 4]
            n += 1
            P.dma('sp', xb.t, xw[(8 + qi) * 128:(9 + qi) * 128, cg * 512:(cg + 1) * 512], writes=[xb], dres=xb)

            def mm(e, pp=pp, qi=qi, w=w):
                ins = None
                for kk in range(16):
                    ins = e.matmul(pp.t, lhsT=mixT.t[:, kk, qi * 128:(qi + 1) * 128], rhs=w.t[:, kk, :], start=(kk == 0), stop=(kk == 15))
                return ins
            P.op('pe', mm, reads=[mixT, w], writes=[pp])
            dst = x1.t[:, qi, cg * 512:(cg + 1) * 512]
            P.op('dve', lambda e, pp=pp, dst=dst, cg=cg: e.tensor_tensor(out=dst, in0=pp.t, in1=gtm.t[:, cg * 512:(cg + 1) * 512], op=ALU.mult),
                 reads=[pp, gtm], writes=[x1])
            P.op('dve', lambda e, dst=dst, xb=xb: e.tensor_tensor(out=dst, in0=dst, in1=xb.t, op=ALU.add), reads=[x1, xb], writes=[x1])
    h2T = P.carve("h2T", R_M, [128, 16, 1024], BF16)
    ctx['h2T'] = h2T
    for qi in range(8):
        ctx['norm_to_fm'](x1.t[:, qi, :], x1, h2T, qi * 128, 2, 3, "n2")


def phase_peer(ctx):
    P, PB = ctx['P'], ctx['PB']
    h2T, x1, ident, gtf, ps_tr = ctx['h2T'], ctx['x1'], ctx['ident'], ctx['gtf'], ctx['ps_tr']
    peer_u, peer_v, w_pq, subkT = ctx['peer_u'], ctx['peer_v'], ctx['w_pq'], ctx['subkT']
    Z1, Z2, Z3 = R_W, R_C + 256, R_T
    sA = P.carve("sA", Z1, [128, 8, 8, 128], BF16)
    sB = P.carve("sB", Z1 + 16384, [128, 8, 8, 128], BF16)

    def diag(idx):
        return (dg['3'], dg['3'].t[:, idx, :]) if idx < 36 else (dg['2'], dg['2'].t[:, idx - 36, :])
    dg = {}

    wpq = [P.carve("wpq%d" % i, Z3 + 16384 * i, [128, 16, 512], BF16) for i in range(2)]
    skT = P.carve("skT", Z2, [128, 16, 128], BF16)
    sf = P.carve("sf", Z1 + 32768, [128, 8, 2, 128], F32)
    qpT = [P.carve("qpT%d" % i, Z1 + 32768 + 8192 + 2048 * i, [128, 1024], BF16) for i in range(2)]
    tk = P.carve("tk", Z1 + 32768 + 12288, [128, 1024], F32)
    ck = P.carve("ck", Z2 + 4096, [128, 8, 8], F32)
    eck = P.carve("eck", Z2 + 4096 + 256, [128, 8, 8], F32)
    P.dma('pool', skT.t, subkT.rearrange("j d n -> d j n"), writes=[skT], dres=skT)
    psq = [PB("psq%d" % i, i, [128, 512]) for i in range(2)]
    pss = [PB("pss%d" % i, 2 + i, [128, 512]) for i in range(2)]
    n_q = 0
    t1, t2, t3 = tk.t[:, 0:16], tk.t[:, 16:32], tk.t[:, 32:48]
    cand = tk.t[:, 64:320]
    tmp = tk.t[:, 320:448]
    cand2 = tk.t[:, 320:576]
    T3 = tk.t[:, 640:768].rearrange("p (a b) -> p a b", a=8)
    E3 = tk.t[:, 768:896].rearrange("p (a b) -> p a b", a=8)
    Zv = tk.t[:, 896:904]
    C1 = tk.t[:, 904:912]
    NEG = -1.0e30
    for cg in range(4):
        w = wpq[cg % 2]
        P.dma('pool', w.t, w_pq[:, cg * 512:(cg + 1) * 512].rearrange("(k p) c -> p k c", p=128), writes=[w], dres=w)
        for hl in range(2):
            h = cg * 2 + hl
            for c in range(2):
                jj = hl * 2 + c
                j = cg * 4 + jj
                qb = qpT[j % 2]
                for th in range(2):
                    pq = psq[n_q % 2]
                    n_q += 1

                    def mm(e, pq=pq, w=w, jj=jj, th=th):
                        ins = None
                        for kk in range(16):
                            ins = e.matmul(pq.t, lhsT=w.t[:, kk, jj * 128:(jj + 1) * 128], rhs=h2T.t[:, kk, th * 512:(th + 1) * 512],
                                           start=(kk == 0), stop=(kk == 15))
                        return ins
                    P.op('pe', mm, reads=[w, h2T], writes=[pq])
                    P.op('act', lambda e, pq=pq, qb=qb, th=th: e.copy(out=qb.t[:, th * 512:(th + 1) * 512], in_=pq.t), reads=[pq], writes=[qb])
                for tq in range(2):
                    pz = pss[(j * 2 + tq) % 2]

                    def mm2(e, pz=pz, qb=qb, tq=tq, j=j):
                        ins = None
                        for t4 in range(4):
                            tt = tq * 4 + t4
                            ins = e.matmul(pz.t[:, t4 * 128:(t4 + 1) * 128], lhsT=qb.t[:, tt * 128:(tt + 1) * 128], rhs=skT.t[:, j, :],
                                           start=True, stop=True)
                        return ins
                    P.op('pe', mm2, reads=[qb, skT], writes=[pz])
                    P.op('act', lambda e, pz=pz, tq=tq, c=c: e.copy(out=sf.t[:, tq * 4:(tq + 1) * 4, c, :],
                                                                   in_=pz.t.rearrange("p (t n) -> p t n", t=4)), reads=[pz], writes=[sf])
            for tt in range(8):
                for c, tdst in ((0, t1), (1, t2)):
                    src = sf.t[:, tt, c, :]
                    P.op('dve', lambda e, src=src, tdst=tdst: e.max(out=tdst[:, 0:8], in_=src), reads=[sf], writes=[tk])
                    P.op('dve', lambda e, src=src, tdst=tdst: e.match_replace(out=tmp, in_to_replace=tdst[:, 0:8], in_values=src, imm_value=NEG),
                         reads=[sf, tk], writes=[tk])
                    P.op('dve', lambda e, tdst=tdst: e.max(out=tdst[:, 8:16], in_=tmp), reads=[tk], writes=[tk])
                P.op('dve', lambda e: e.tensor_tensor(out=cand.rearrange("p (a b) -> p a b", a=16), in0=t1.unsqueeze(2).to_broadcast([128, 16, 16]),
                                                      in1=t2.unsqueeze(1).to_broadcast([128, 16, 16]), op=ALU.add), reads=[tk], writes=[tk])
                P.op('dve', lambda e: e.max(out=t3[:, 0:8], in_=cand), reads=[tk], writes=[tk])
                P.op('dve', lambda e: e.match_replace(out=cand2, in_to_replace=t3[:, 0:8], in_values=cand, imm_value=NEG), reads=[tk], writes=[tk])
                P.op('dve', lambda e: e.max(out=t3[:, 8:16], in_=cand2), reads=[tk], writes=[tk])
                P.op('dve', lambda e, tt=tt: e.tensor_copy(out=T3[:, tt, :], in_=t3), reads=[tk], writes=[tk])
            P.op('dve', lambda e: e.tensor_tensor(out=E3, in0=T3, in1=T3[:, :, 0:1].to_broadcast([128, 8, 16]), op=ALU.subtract),
                 reads=[tk], writes=[tk])
            P.op('act', lambda e: e.activation(out=E3, in_=E3, func=AF.Exp), reads=[tk], writes=[tk])
            P.op('dve', lambda e: e.tensor_reduce(out=Zv, in_=E3, axis=AX.X, op=ALU.add), reads=[tk], writes=[tk])
            P.op('act', lambda e: e.activation(out=Zv, in_=Zv, func=AF.Ln), reads=[tk], writes=[tk])
            P.op('dve', lambda e: e.tensor_tensor(out=C1, in0=T3[:, :, 15], in1=T3[:, :, 0], op=ALU.subtract), reads=[tk], writes=[tk])
            P.op('dve', lambda e, h=h: e.tensor_tensor(out=ck.t[:, :, h], in0=C1, in1=Zv, op=ALU.subtract), reads=[tk], writes=[ck])
            P.op('dve', lambda e, h=h: e.tensor_tensor(out=sA.t[:, :, h, :], in0=sf.t[:, :, 0, :],
                                                       in1=T3[:, :, 15:16].to_broadcast([128, 8, 128]), op=ALU.subtract),
                 reads=[sf, tk], writes=[sA])
            P.op('act', lambda e, h=h: e.copy(out=sB.t[:, :, h, :], in_=sf.t[:, :, 1, :]), reads=[sf], writes=[sB])
    P.op('act', lambda e: e.activation(out=eck.t, in_=ck.t, func=AF.Exp), reads=[ck], writes=[eck])
    dg['3'] = P.carve("dg3", Z3 + 30720, [128, 36, 128], BF16)
    dg['2'] = P.carve("dg2", Z2 + 8192, [128, 28, 128], BF16)
    assert Z3 + 30720 + 36 * 256 <= ARENA and Z2 + 8192 + 28 * 256 <= R_C + C_GTF
    for idx in range(64):
        tt, h = idx // 8, idx % 8
        dres, dap = diag(idx)
        P.op('dve', lambda e, dap=dap, tt=tt, h=h: e.tensor_scalar(out=dap, in0=ident.t, scalar1=eck.t[:, tt, h:h + 1], scalar2=None, op0=ALU.mult),
             reads=[ident, eck], writes=[dres])
    vring = [P.carve("vr%d" % i, Z1 + 32768 + 4096 * i, [128, 2048], BF16) for i in range(4)]
    vring.append(P.carve("vr4", Z2 + 4096, [128, 2048], BF16))
    vring.append(P.carve("vr5", Z3 + 4096, [128, 2048], BF16))
    ubuf = P.carve("ubuf", Z2, [128, 2048], BF16)
    uTb = P.carve("uTb", Z3, [128, 16, 128], BF16)
    biasc = [P.carve("biasc%d" % i, Z2 + 15360 + 256 * i, [128, 8, 8], F32) for i in range(2)]
    NEG_ = 4
    EG = [P.carve("EG%d" % i, Z3 + 8192 + 2048 * i, [128, 8, 128], BF16) for i in range(NEG_)]
    WT = [P.carve("WT%d" % i, Z3 + 16384 + 2048 * i, [128, 1024], BF16) for i in range(5)]
    gelb = [P.carve("gel%d" % i, Z3 + 26624 + 2048 * i, [128, 1024], BF16) for i in range(2)]
    NV = 6
    NW = 5

    psA = [PB("psA%d" % i, i, [128, 512]) for i in range(2)]
    psG = [PB("psG%d" % i, 2 + i, [128, 512]) for i in range(2)]
    psY = [PB("psY%d" % i, 4 + i, [128, 512]) for i in range(2)] + [PB("psY2", 7, [128, 512])]
    n_y = {'i': 0}

    def emit_load_u(ec):
        P.dma('pool', ubuf.t, peer_u[ec * 128:(ec + 1) * 128, :], writes=[ubuf], dres=ubuf)

    def emit_load_v(ec):
        vb = vring[ec % NV]
        P.dma('pool', vb.t, peer_v[ec * 128:(ec + 1) * 128, :], writes=[vb], dres=vb)

    def emit_vscale(ec):
        vb = vring[ec % NV]
        P.op('dve', lambda e, vb=vb: e.tensor_tensor(out=vb.t, in0=vb.t, in1=gtf.t, op=ALU.mult), reads=[vb, gtf], writes=[vb])

    def items_stageA(ec):
        items = []
        for half in range(2):
            def it(half=half):
                tr = ps_tr[0]

                def trf(e):
                    ins = None
                    for kk in range(8):
                        k = half * 8 + kk
                        ins = e.transpose(tr.t[:, kk, :], ubuf.t[:, k * 128:(k + 1) * 128], ident.t)
                    return ins
                P.op('pe', trf, reads=[ubuf, ident], writes=[tr])
                P.op('dve', lambda e: e.tensor_copy(out=uTb.t[:, half * 8:(half + 1) * 8, :], in_=tr.t), reads=[tr], writes=[uTb])
                if half == 1 and ec + 1 < 128:
                    emit_load_u(ec + 1)
            items.append(it)
        for th in range(2):
            for part in range(2):
                def it(th=th, part=part):
                    pa = psA[th]

                    def amm(e):
                        ins = None
                        for k in range(part * 8, part * 8 + 8):
                            ins = e.matmul(pa.t, lhsT=uTb.t[:, k, :], rhs=h2T.t[:, k, th * 512:(th + 1) * 512], start=(k == 0), stop=(k == 15))
                        return ins
                    P.op('pe', amm, reads=[uTb, h2T], writes=[pa])
                items.append(it)
        items.append(lambda: emit_gelu(ec))
        return items

    def emit_gelu(ec):
        gel = gelb[ec % 2]
        for th in range(2):
            P.op('act', lambda e, th=th: e.activation(out=gel.t[:, th * 512:(th + 1) * 512], in_=psA[th].t, func=AF.Gelu),
                 reads=[psA[th]], writes=[gel])

    def items_y(ec0):
        pair = [(WT[ec0 % NW], vring[ec0 % NV]), (WT[(ec0 + 1) % NW], vring[(ec0 + 1) % NV])]
        items = []
        for tt in range(8):
            for cb_ in range(4):
                def it(tt=tt, cb_=cb_):
                    py = psY[n_y['i'] % 3]
                    n_y['i'] += 1

                    def ymm(e):
                        ins = None
                        for i, (w_, v_) in enumerate(pair):
                            ins = e.matmul(py.t, lhsT=w_.t[:, tt * 128:(tt + 1) * 128], rhs=v_.t[:, cb_ * 512:(cb_ + 1) * 512],
                                           start=(i == 0), stop=(i == 1))
                        return ins
                    P.op('pe', ymm, reads=[pair[0][0], pair[0][1], pair[1][0], pair[1][1]], writes=[py])
                    dst = x1.t[:, tt, cb_ * 512:(cb_ + 1) * 512]
                    P.op('dve', lambda e: e.tensor_tensor(out=dst, in0=dst, in1=py.t, op=ALU.add), reads=[x1, py], writes=[x1])
                items.append(it)
        return items

    def emit_bias(ec):
        b_ = biasc[ec % 2]
        P.op('dve', lambda e: e.tensor_copy(out=b_.t, in_=sA.t[:, :, :, ec]), reads=[sA], writes=[b_])

    def gate_elem(g):
        ec, tt = g // 8, g % 8
        eg_ = EG[g % NEG_]
        b_ = biasc[ec % 2]

        def ex(e):
            ins = None
            for h in range(8):
                ins = e.activation(out=eg_.t[:, h, :], in_=sB.t[:, tt, h, :], func=AF.Exp, bias=b_.t[:, tt, h:h + 1])
            return ins
        P.op('act', ex, reads=[sB, b_], writes=[eg_])
        P.op('dve', lambda e: e.scalar_tensor_tensor(out=eg_.t, in0=eg_.t, scalar=1.0, in1=eg_.t, op0=ALU.is_ge, op1=ALU.mult),
             reads=[eg_], writes=[eg_])

    def gate_pe(g):
        ec, tt = g // 8, g % 8
        g_ = EG[g % NEG_]
        pg = psG[tt // 4]
        rds = [g_]
        daps = []
        for h in range(8):
            dres, dap = diag(tt * 8 + h)
            daps.append(dap)
            if dres not in rds:
                rds.append(dres)

        def gmm(e):
            ins = None
            for h in range(8):
                ins = e.matmul(pg.t[:, (tt % 4) * 128:(tt % 4 + 1) * 128], lhsT=g_.t[:, h, :], rhs=daps[h], start=(h == 0), stop=(h == 7))
            return ins
        P.op('pe', gmm, reads=rds, writes=[pg])

    def emit_WT_half(ec, th):
        wt = WT[ec % NW]
        gel = gelb[ec % 2]
        P.op('dve', lambda e: e.tensor_tensor(out=wt.t[:, th * 512:(th + 1) * 512], in0=psG[th].t,
                                              in1=gel.t[:, th * 512:(th + 1) * 512], op=ALU.mult),
             reads=[psG[th], gel], writes=[wt])

    emit_load_u(0)
    emit_load_v(0)
    emit_vscale(0)
    for it in items_stageA(0):
        it()
    emit_bias(0)
    emit_bias(1)
    gate_elem(0)
    gate_elem(1)
    gate_elem(2)
    ypend = []
    for ec in range(128):
        if ec + 1 < 128:
            emit_load_v(ec + 1)
        SA = items_stageA(ec + 1) if ec + 1 < 128 else []
        if ec % 2 == 0 and ec >= 2:
            ypend = items_y(ec - 2)
        ny = min(16, len(ypend))
        YL = ypend[:ny]
        ypend = ypend[ny:]
        plan = [[] for _ in range(8)]
        for k, it in enumerate(SA):
            plan[min(7, k + 1)].append(it)
        yq = list(YL)
        quota = [3, 2, 2, 2, 2, 2, 2, 1]
        for tt in range(8):
            for _ in range(quota[tt]):
                if yq:
                    plan[tt].append(yq.pop(0))
        plan[7].extend(yq)
        for tt in range(8):
            g = ec * 8 + tt
            gate_pe(g)
            if g + 3 < 1024:
                gate_elem(g + 3)
            if tt == 4:
                emit_WT_half(ec, 0)
            for it in plan[tt]:
                it()
        emit_WT_half(ec, 1)
        if ec + 2 < 128:
            emit_bias(ec + 2)
        if ec + 1 < 128:
            emit_vscale(ec + 1)
    for it in ypend:
        it()
    for it in items_y(126):
        it()


def phase_final(ctx):
    P, PB, x1, csbf, y = ctx['P'], ctx['PB'], ctx['x1'], ctx['csbf'], ctx['y']
    w_adaf, b_adaf, gfin = ctx['w_adaf'], ctx['b_adaf'], ctx['gfin']
    Z1 = R_W
    wb = [P.carve("wf%d" % i, Z1 + 16384 * i, [128, 16, 512], BF16) for i in range(2)]
    sho = P.carve("sho", Z1 + 32768, [128, 2048], F32)
    gsc = P.carve("gsc", Z1 + 40960, [128, 2048], F32)
    Z3 = R_T
    brow = P.carve("fbrow", Z3, [1, 512], F32)
    mrow = P.carve("fmrow", Z3 + 2048, [1, 512], F32)
    grow = P.carve("fgrow", Z3 + 4096, [1, 2048], F32)
    one1 = P.carve("fone1", Z3 + 12288, [1, 128], F32)
    fss = P.carve("fss", Z3 + 12800, [128, 4], F32)
    fjk = P.carve("fjk", Z3 + 12816, [128, 2048], BF16)
    ot = [P.carve("ot%d" % i, Z3 + 16912 + 8192 * i, [128, 2048], F32) for i in range(2)]
    ps_row = PB("f_row", 0, [1, 512])
    ps_bc = PB("f_bc", 1, [128, 512])
    P.op('dve', lambda e: e.memset(one1.t, 1.0), writes=[one1])
    P.dma('sp', grow.t, gfin, writes=[grow], dres=grow)
    for g in range(8):
        w = wb[g % 2]
        P.dma('pool', w.t, w_adaf[:, g * 512:(g + 1) * 512].rearrange("(k p) c -> p k c", p=128), writes=[w], dres=w)
        P.dma('sp', brow.t, b_adaf[:, g * 512:(g + 1) * 512], writes=[brow], dres=brow)

        def mm(e, w=w):
            ins = None
            for k in range(16):
                ins = e.matmul(ps_row.t, lhsT=csbf.t[:, k:k + 1], rhs=w.t[:, k, :], start=(k == 0), stop=(k == 15))
            return ins
        P.op('pe', mm, reads=[csbf, w], writes=[ps_row])
        sub = g % 4
        if g < 4:
            P.op('dve', lambda e: e.tensor_tensor(out=mrow.t, in0=ps_row.t, in1=brow.t, op=ALU.add), reads=[ps_row, brow], writes=[mrow])
            dst = sho.t[:, sub * 512:(sub + 1) * 512]
            dres = sho
        else:
            P.op('dve', lambda e: e.scalar_tensor_tensor(out=mrow.t, in0=ps_row.t, scalar=1.0, in1=brow.t, op0=ALU.add, op1=ALU.add),
                 reads=[ps_row, brow], writes=[mrow])
            P.op('dve', lambda e, sub=sub: e.tensor_tensor(out=mrow.t, in0=mrow.t, in1=grow.t[0:1, sub * 512:(sub + 1) * 512], op=ALU.mult),
                 reads=[mrow, grow], writes=[mrow])
            dst = gsc.t[:, sub * 512:(sub + 1) * 512]
            dres = gsc
        P.op('pe', lambda e: e.matmul(ps_bc.t, lhsT=one1.t[0:1, 0:128], rhs=mrow.t[0:1, :], start=True, stop=True), reads=[mrow, one1], writes=[ps_bc])
        P.op('act', lambda e, dst=dst: e.copy(out=dst, in_=ps_bc.t), reads=[ps_bc], writes=[dres])
    for tt in range(8):
        src = x1.t[:, tt, :]
        o = ot[tt % 2]
        P.op('act', lambda e, src=src: e.activation(out=fjk.t, in_=src, func=AF.Square, accum_out=fss.t[:, 0:1]), reads=[x1], writes=[fjk, fss])
        P.op('dve', lambda e: e.tensor_scalar(out=fss.t[:, 1:2], in0=fss.t[:, 0:1], scalar1=1.0 / D, scalar2=EPS, op0=ALU.mult, op1=ALU.add),
             reads=[fss], writes=[fss])
        P.op('act', lambda e: e.activation(out=fss.t[:, 3:4], in_=fss.t[:, 1:2], func=AF.Sqrt), reads=[fss], writes=[fss])
        P.op('dve', lambda e: e.reciprocal(out=fss.t[:, 2:3], in_=fss.t[:, 3:4]), reads=[fss], writes=[fss])
        P.op('dve', lambda e, src=src, o=o: e.scalar_tensor_tensor(out=o.t, in0=src, scalar=fss.t[:, 2:3], in1=gsc.t, op0=ALU.mult, op1=ALU.mult),
             reads=[x1, fss, gsc], writes=[o])
        P.op('dve', lambda e, o=o: e.tensor_tensor(out=o.t, in0=o.t, in1=sho.t, op=ALU.add), reads=[o, sho], writes=[o])
        P.dma('sp', y[tt * 128:(tt + 1) * 128, :], o.t, reads=[o], dres=o, is_out=True)


_CACHE = {}


def kernel(**inputs):
    maps = host_inputs(**inputs)
    if 'nc' not in _CACHE:
        ctx = build_program()
        phase_mixers(ctx)
        phase_mlstm(ctx)
        phase_wout_norm2(ctx)
        phase_peer(ctx)
        phase_final(ctx)
        ctx['P'].finish()
        _CACHE['nc'] = ctx['nc']
    res = run_bass_kernel_spmd(_CACHE['nc'], maps, core_ids=list(range(8)))
    out = np.zeros((4, S, D), dtype=np.float32)
    for core in range(8):
        b, hf = core // 2, core % 2
        out[b, hf * 1024:(hf + 1) * 1024] = res.results[core]["y"]
    return out
```
